# Optimizing a Trainium2 kernel written in Bass

```python
import math
import jax, jax.numpy as jnp
from jax import lax
import numpy as np

D_MODEL = 1024
BATCH = 8
SEQ = 2048
DEPTH = 2

SSD_EXPAND = 2
SSD_D_INNER = SSD_EXPAND * D_MODEL
SSD_HEAD_DIM = 64
SSD_N_HEADS = SSD_D_INNER // SSD_HEAD_DIM
SSD_N_GROUPS = 4
SSD_HEADS_PER_GROUP = SSD_N_HEADS // SSD_N_GROUPS
SSD_D_STATE = 128
SSD_CONV_WIDTH = 4
SSD_CHUNK = 128
SSD_CONV_DIM = SSD_D_INNER + 2 * SSD_N_GROUPS * SSD_D_STATE

S5_WIDTH = D_MODEL
S5_GROUP = 16
S5_N_GROUPS = S5_WIDTH // S5_GROUP
S5_STATE = 64
S5_MIN_STEP = 0.001
S5_MAX_STEP = 0.1
S5_MAX_REAL = -1e-4

FFN_HIDDEN = 2816

RMS_EPS = 1e-6

IN_PROJ_DIM = SSD_D_INNER + SSD_CONV_DIM + SSD_N_HEADS + S5_WIDTH + 2 * D_MODEL
SPLITS = list(np.cumsum([SSD_D_INNER, SSD_CONV_DIM, SSD_N_HEADS, S5_WIDTH, D_MODEL]))

kernel_name = "hybrid_macaron_gated_ssd_s5"


def rmsnorm(x, g):
    xf = x.astype(jnp.float32)
    xf = xf * lax.rsqrt(jnp.mean(xf * xf, axis=-1, keepdims=True) + RMS_EPS)
    return (xf * g.astype(jnp.float32)).astype(x.dtype)


def swiglu(x, w_gate, w_up, w_down):
    return (jax.nn.silu(x @ w_gate) * (x @ w_up)) @ w_down


def causal_depthwise_conv(u, w, b):
    k, c = w.shape
    out = lax.conv_general_dilated(
        u, w[:, None, :].astype(u.dtype), window_strides=(1,), padding=[(k - 1, 0)],
        dimension_numbers=("NWC", "WIO", "NWC"), feature_group_count=c)
    return out + b.astype(u.dtype)


def segsum(a):
    t = a.shape[-1]
    cs = jnp.cumsum(a, axis=-1)
    seg = cs[..., :, None] - cs[..., None, :]
    mask = jnp.tril(jnp.ones((t, t), dtype=bool))
    return jnp.where(mask, seg, -jnp.inf)


def ssd_chunked(xdt, adt, bmat, cmat):
    b, l, h, p = xdt.shape
    g, n = bmat.shape[2], bmat.shape[3]
    r = h // g
    q = SSD_CHUNK
    c = l // q
    X = xdt.reshape(b, c, q, g, r, p)
    A = adt.reshape(b, c, q, g, r).transpose(0, 3, 4, 1, 2)
    Bc = bmat.reshape(b, c, q, g, n)
    Cc = cmat.reshape(b, c, q, g, n)
    a_cum = jnp.cumsum(A, axis=-1)
    decay_in = jnp.exp(segsum(A))
    cb = jnp.einsum("bclgn,bcsgn->bcgls", Cc, Bc)
    y_diag = jnp.einsum("bcgls,bgrcls,bcsgrp->bclgrp", cb, decay_in, X)
    decay_states = jnp.exp(a_cum[..., -1:] - a_cum)
    states = jnp.einsum("bclgn,bgrcl,bclgrp->bcgrpn", Bc, decay_states, X)
    states = jnp.concatenate([jnp.zeros_like(states[:, :1]), states], axis=1)
    chunk_a = jnp.pad(a_cum[..., -1], ((0, 0), (0, 0), (0, 0), (1, 0)))
    chunk_decay = jnp.exp(segsum(chunk_a))
    new_states = jnp.einsum("bgrzc,bcgrpn->bzgrpn", chunk_decay, states)
    states = new_states[:, :-1]
    y_off = jnp.einsum("bclgn,bcgrpn,bgrcl->bclgrp", Cc, states, jnp.exp(a_cum))
    return (y_diag + y_off).reshape(b, l, h, p)


def ssd_branch(z, xbc, dt_raw, conv_w, conv_b, dt_bias, a_log, d_skip, norm_g):
    b, l, _ = z.shape
    xbc = jax.nn.silu(causal_depthwise_conv(xbc, conv_w, conv_b)).astype(jnp.float32)
    xs = xbc[..., :SSD_D_INNER].reshape(b, l, SSD_N_HEADS, SSD_HEAD_DIM)
    bm = xbc[..., SSD_D_INNER:SSD_D_INNER + SSD_N_GROUPS * SSD_D_STATE].reshape(b, l, SSD_N_GROUPS, SSD_D_STATE)
    cm = xbc[..., SSD_D_INNER + SSD_N_GROUPS * SSD_D_STATE:].reshape(b, l, SSD_N_GROUPS, SSD_D_STATE)
    dt = jax.nn.softplus(dt_raw.astype(jnp.float32) + dt_bias.astype(jnp.float32))
    a = -jnp.exp(a_log.astype(jnp.float32))
    y = ssd_chunked(xs * dt[..., None], a * dt, bm, cm) + d_skip.astype(jnp.float32)[:, None] * xs
    y = y.reshape(b, l, SSD_D_INNER)
    y = y * jax.nn.silu(z.astype(jnp.float32))
    yg = y.reshape(b, l, SSD_N_GROUPS, SSD_D_INNER // SSD_N_GROUPS)
    yg = yg * lax.rsqrt(jnp.mean(yg * yg, axis=-1, keepdims=True) + RMS_EPS)
    y = yg.reshape(b, l, SSD_D_INNER) * norm_g.astype(jnp.float32)
    return y.astype(z.dtype)


def s5_branch(u, lam_re, lam_im, b_re, b_im, c_re, c_im, log_step, d_skip, w_glu):
    b, l, _ = u.shape
    uf = u.astype(jnp.float32)
    ug = uf.reshape(b, l, S5_N_GROUPS, S5_GROUP)
    lam = lax.complex(jnp.minimum(lam_re.astype(jnp.float32), S5_MAX_REAL), lam_im.astype(jnp.float32))
    step = jnp.exp(log_step.astype(jnp.float32))[:, None]
    lam_bar = jnp.exp(lam * step)
    b_bar = ((lam_bar - 1.0) / lam)[..., None] * lax.complex(b_re.astype(jnp.float32), b_im.astype(jnp.float32))
    bu = jnp.einsum("blgc,gnc->blgn", ug.astype(jnp.complex64), b_bar)
    a_elems = jnp.broadcast_to(lam_bar, bu.shape)

    def combine(e1, e2):
        a1, s1 = e1
        a2, s2 = e2
        return a2 * a1, a2 * s1 + s2

    _, states = lax.associative_scan(combine, (a_elems, bu), axis=1)
    c = lax.complex(c_re.astype(jnp.float32), c_im.astype(jnp.float32))
    y = jnp.real(jnp.einsum("blgn,gcn->blgc", states, c)).reshape(b, l, S5_WIDTH)
    y = y + d_skip.astype(jnp.float32) * uf
    y = jax.nn.gelu(y).astype(u.dtype)
    val, gate = jnp.split(y @ w_glu, 2, axis=-1)
    return val * jax.nn.sigmoid(gate)


def mixer(u, w_in, ssd_conv_w, ssd_conv_b, ssd_dt_bias, ssd_a_log, ssd_d, ssd_norm_g, w_branch_a,
          s5_lambda_re, s5_lambda_im, s5_b_re, s5_b_im, s5_c_re, s5_c_im, s5_log_step, s5_d, s5_w_glu,
          w_branch_b, w_out):
    proj = u @ w_in
    z, xbc, dt_raw, u5, g_a, g_b = jnp.split(proj, SPLITS, axis=-1)
    y_a = ssd_branch(z, xbc, dt_raw, ssd_conv_w, ssd_conv_b, ssd_dt_bias, ssd_a_log, ssd_d, ssd_norm_g) @ w_branch_a
    y_b = s5_branch(u5, s5_lambda_re, s5_lambda_im, s5_b_re, s5_b_im, s5_c_re, s5_c_im,
                    s5_log_step, s5_d, s5_w_glu) @ w_branch_b
    merged = jax.nn.sigmoid(g_a) * y_a + jax.nn.sigmoid(g_b) * y_b
    return merged @ w_out


def setup_inputs(seed: int = 0) -> dict:
    key = jax.random.key(seed)
    ks = iter(jax.random.split(key, 40))
    f32 = jnp.float32
    L = DEPTH

    def normal(shape, scale):
        return jax.random.normal(next(ks), shape, f32) * scale

    def gain(shape):
        return 1.0 + 0.02 * jax.random.normal(next(ks), shape, f32)

    x = jax.random.normal(next(ks), (BATCH, SEQ, D_MODEL), f32)
    d_ = D_MODEL
    dt_init = jnp.exp(jax.random.uniform(next(ks), (L, SSD_N_HEADS), f32,
                                         math.log(0.001), math.log(0.1)))
    n_idx = jnp.arange(S5_STATE, dtype=f32)
    return {
        "x": x,
        "ffn1_pre_g": gain((L, d_)),
        "ffn1_post_g": gain((L, d_)),
        "ffn1_w_gate": normal((L, d_, FFN_HIDDEN), d_ ** -0.5),
        "ffn1_w_up": normal((L, d_, FFN_HIDDEN), d_ ** -0.5),
        "ffn1_w_down": normal((L, FFN_HIDDEN, d_), FFN_HIDDEN ** -0.5),
        "mix_pre_g": gain((L, d_)),
        "mix_post_g": gain((L, d_)),
        "w_in": normal((L, d_, IN_PROJ_DIM), d_ ** -0.5),
        "ssd_conv_w": normal((L, SSD_CONV_WIDTH, SSD_CONV_DIM), SSD_CONV_WIDTH ** -0.5),
        "ssd_conv_b": normal((L, SSD_CONV_DIM), 0.02),
        "ssd_dt_bias": dt_init + jnp.log(-jnp.expm1(-dt_init)),
        "ssd_a_log": jnp.log(jax.random.uniform(next(ks), (L, SSD_N_HEADS), f32, 1.0, 16.0)),
        "ssd_d": gain((L, SSD_N_HEADS)),
        "ssd_norm_g": gain((L, SSD_D_INNER)),
        "w_branch_a": normal((L, SSD_D_INNER, d_), SSD_D_INNER ** -0.5),
        "s5_lambda_re": -0.5 + 0.01 * jax.random.normal(next(ks), (L, S5_N_GROUPS, S5_STATE), f32),
        "s5_lambda_im": math.pi * n_idx + 0.01 * jax.random.normal(next(ks), (L, S5_N_GROUPS, S5_STATE), f32),
        "s5_b_re": normal((L, S5_N_GROUPS, S5_STATE, S5_GROUP), (2 * S5_GROUP) ** -0.5),
        "s5_b_im": normal((L, S5_N_GROUPS, S5_STATE, S5_GROUP), (2 * S5_GROUP) ** -0.5),
        "s5_c_re": normal((L, S5_N_GROUPS, S5_GROUP, S5_STATE), (2 * S5_STATE) ** -0.5),
        "s5_c_im": normal((L, S5_N_GROUPS, S5_GROUP, S5_STATE), (2 * S5_STATE) ** -0.5),
        "s5_log_step": jax.random.uniform(next(ks), (L, S5_N_GROUPS), f32,
                                          math.log(S5_MIN_STEP), math.log(S5_MAX_STEP)),
        "s5_d": normal((L, S5_WIDTH), 1.0),
        "s5_w_glu": normal((L, S5_WIDTH, 2 * S5_WIDTH), S5_WIDTH ** -0.5),
        "w_branch_b": normal((L, S5_WIDTH, d_), S5_WIDTH ** -0.5),
        "w_out": normal((L, d_, d_), d_ ** -0.5),
        "ffn2_pre_g": gain((L, d_)),
        "ffn2_post_g": gain((L, d_)),
        "ffn2_w_gate": normal((L, d_, FFN_HIDDEN), d_ ** -0.5),
        "ffn2_w_up": normal((L, d_, FFN_HIDDEN), d_ ** -0.5),
        "ffn2_w_down": normal((L, FFN_HIDDEN, d_), FFN_HIDDEN ** -0.5),
    }


def reference(x, ffn1_pre_g, ffn1_post_g, ffn1_w_gate, ffn1_w_up, ffn1_w_down,
              mix_pre_g, mix_post_g, w_in, ssd_conv_w, ssd_conv_b, ssd_dt_bias, ssd_a_log, ssd_d,
              ssd_norm_g, w_branch_a, s5_lambda_re, s5_lambda_im, s5_b_re, s5_b_im, s5_c_re, s5_c_im,
              s5_log_step, s5_d, s5_w_glu, w_branch_b, w_out,
              ffn2_pre_g, ffn2_post_g, ffn2_w_gate, ffn2_w_up, ffn2_w_down):
    h = x
    for i in range(DEPTH):
        f = swiglu(rmsnorm(h, ffn1_pre_g[i]), ffn1_w_gate[i], ffn1_w_up[i], ffn1_w_down[i])
        h = h + 0.5 * rmsnorm(f, ffn1_post_g[i])
        m = mixer(rmsnorm(h, mix_pre_g[i]), w_in[i], ssd_conv_w[i], ssd_conv_b[i], ssd_dt_bias[i],
                  ssd_a_log[i], ssd_d[i], ssd_norm_g[i], w_branch_a[i],
                  s5_lambda_re[i], s5_lambda_im[i], s5_b_re[i], s5_b_im[i], s5_c_re[i], s5_c_im[i],
                  s5_log_step[i], s5_d[i], s5_w_glu[i], w_branch_b[i], w_out[i])
        h = h + rmsnorm(m, mix_post_g[i])
        f = swiglu(rmsnorm(h, ffn2_pre_g[i]), ffn2_w_gate[i], ffn2_w_up[i], ffn2_w_down[i])
        h = h + 0.5 * rmsnorm(f, ffn2_post_g[i])
    return h
```

```python
import math
from contextlib import ExitStack
import numpy as np
import concourse.bass as bass
import concourse.mybir as mybir
from concourse.bass_utils import run_bass_kernel_spmd

F32 = mybir.dt.float32
BF16 = mybir.dt.bfloat16
I32 = mybir.dt.int32
AF = mybir.ActivationFunctionType
ALU = mybir.AluOpType

NTOK = 2048
TT = 512
NTILE = NTOK // TT
D = 1024
FF = 2816
FC = FF // 128
DEPTH = 2
INPROJ = 8224
OFF_Z, OFF_X, OFF_B, OFF_C, OFF_DT, OFF_U5, OFF_GA, OFF_GB = 0, 2048, 4096, 4608, 5120, 5152, 6176, 7200
EPS = 1e-6
TWO_PI = 2.0 * math.pi

PARAM_SHAPES = {
    "ffn1_pre_g": (2, 1024), "ffn1_post_g": (2, 1024), "ffn1_w_gate": (2, 1024, 2816), "ffn1_w_up": (2, 1024, 2816),
    "ffn1_w_down": (2, 2816, 1024), "mix_pre_g": (2, 1024), "mix_post_g": (2, 1024), "w_in": (2, 1024, 8224),
    "ssd_conv_w": (2, 4, 3072), "ssd_conv_b": (2, 3072), "ssd_dt_bias": (2, 32), "ssd_a_log": (2, 32),
    "ssd_d": (2, 32), "ssd_norm_g": (2, 2048), "w_branch_a": (2, 2048, 1024), "s5_lambda_re": (2, 64, 64),
    "s5_lambda_im": (2, 64, 64), "s5_b_re": (2, 64, 64, 16), "s5_b_im": (2, 64, 64, 16), "s5_c_re": (2, 64, 16, 64),
    "s5_c_im": (2, 64, 16, 64), "s5_log_step": (2, 64), "s5_d": (2, 1024), "s5_w_glu": (2, 1024, 2048),
    "w_branch_b": (2, 1024, 1024), "w_out": (2, 1024, 1024), "ffn2_pre_g": (2, 1024), "ffn2_post_g": (2, 1024),
    "ffn2_w_gate": (2, 1024, 2816), "ffn2_w_up": (2, 1024, 2816), "ffn2_w_down": (2, 2816, 1024),
}


class Sched:
    def __init__(self, nc):
        self.nc = nc
        self.e = {"pe": nc.tensor, "act": nc.scalar, "dve": nc.vector, "pool": nc.gpsimd, "sp": nc.sync}
        self.sems = []
        self.semi = {}
        self.cnt = {}
        for k in ("pe", "act", "dve", "pool"):
            self._newsem(k)
        self.waited = {k: {} for k in self.e}
        self.lastw = {}
        self.readers = {}
        self.dmasem = {}
        self.ninstr = {k: 0 for k in self.e}
        self.pending = []
        self.nobarrier = set()
        self.verbose = False
        self.phase_name = ''

    def _alloc(self, name):
        self.sems.append(self.nc.alloc_semaphore(name))
        return len(self.sems) - 1

    def _newsem(self, k):
        self.semi[k] = self._alloc(f"sem_{k}_{len(self.sems)}")
        self.cnt[k] = 0

    def _deps(self, reads, writes):
        toks = []
        for k in reads:
            t = self.lastw.get(k)
            if t is not None:
                toks.append(t)
        for k in writes:
            t = self.lastw.get(k)
            if t is not None:
                toks.append(t)
            r = self.readers.get(k)
            if r:
                toks.extend((si, v, src) for (si, (v, src)) in r.items())
        return toks

    def _waits(self, eng, toks):
        need = {}
        for (si, v, src) in toks:
            if eng == "pe" and src == "pe":
                continue
            if need.get(si, 0) < v:
                need[si] = v
        w = self.waited[eng]
        for si, v in need.items():
            if w.get(si, 0) >= v:
                continue
            self.e[eng].wait_ge(self.sems[si], v)
            w[si] = v

    def _record(self, tok, reads, writes):
        ws = set(writes)
        for k in ws:
            self.lastw[k] = tok
            self.readers[k] = {}
        for k in reads:
            if k in ws:
                continue
            r = self.readers.setdefault(k, {})
            if r.get(tok[0], (0, None))[0] < tok[1]:
                r[tok[0]] = (tok[1], tok[2])

    DEFCOST = {"pe": 1.0, "act": 0.5, "dve": 0.65, "pool": 1.2, "sp": 0.05}

    def op(self, eng, fn, reads=(), writes=(), c=None):
        self.pending.append(("op", eng, _snapshot(fn), tuple(reads), tuple(writes),
                             self.DEFCOST[eng] if c is None else c))

    def dma(self, q, out, in_, reads=(), writes=(), key=None, c=None, persistent=False, **kw):
        key = key or (list(writes)[0] if writes else list(reads)[0])
        if persistent:
            self.nobarrier.add(key)
        self.pending.append(("dma", q, (out, in_, kw, key), tuple(reads), tuple(writes), 3.0 if c is None else c))

    def _emit_op(self, eng, fn, reads, writes):
        self._waits(eng, self._deps(reads, writes))
        ins = fn()
        if isinstance(ins, (list, tuple)):
            self.ninstr[eng] += len(ins)
            ins = ins[-1]
        else:
            self.ninstr[eng] += 1
        self.cnt[eng] += 1
        ins.then_inc(self.sems[self.semi[eng]], 1)
        tok = (self.semi[eng], self.cnt[eng], eng)
        self._record(tok, reads, writes)
        if self.cnt[eng] >= 30000:
            self._newsem(eng)

    def _emit_dma(self, q, out, in_, kw, key, reads, writes):
        self._waits(q, self._deps(reads, writes))
        skey = (key, q == "pool")
        if skey not in self.dmasem:
            self.dmasem[skey] = [self._alloc(f"dsem_{len(self.sems)}"), 0, q]
        ds = self.dmasem[skey]
        if ds[1] >= 30000:
            ds[0] = self._alloc(f"dsem_{len(self.sems)}")
            ds[1] = 0
        ds[1] += 16
        self.e[q].dma_start(out=out, in_=in_, **kw).then_inc(self.sems[ds[0]], 16)
        self.ninstr[q] += 1
        self._record((ds[0], ds[1], "dma"), reads, writes)

    def flush(self):
        import heapq
        ops = self.pending
        self.pending = []
        n = len(ops)
        if n == 0:
            return
        lastw, readers = {}, {}
        preds = [set() for _ in range(n)]
        for i, o in enumerate(ops):
            rd, wr = o[3], o[4]
            for k in rd:
                if k in lastw:
                    preds[i].add(lastw[k])
            for k in wr:
                if k in lastw:
                    preds[i].add(lastw[k])
                for j in readers.get(k, ()):
                    preds[i].add(j)
            ws = set(wr)
            for k in ws:
                lastw[k] = i
                readers[k] = []
            for k in rd:
                if k not in ws:
                    readers.setdefault(k, []).append(i)
            preds[i].discard(i)
        succs = [[] for _ in range(n)]
        npred = [0] * n
        for i in range(n):
            npred[i] = len(preds[i])
            for j in preds[i]:
                succs[j].append(i)
        engs = ("pe", "act", "dve", "pool", "sp")
        tail = [0.0] * n
        for i in range(n - 1, -1, -1):
            t = 0.0
            for j in succs[i]:
                if tail[j] > t:
                    t = tail[j]
            tail[i] = t + ops[i][5] + 0.15
        etime = {e: 0.0 for e in engs}
        fut = {e: [] for e in engs}
        now = {e: [] for e in engs}
        ready_t = [0.0] * n
        finish = [0.0] * n
        for i in range(n):
            if npred[i] == 0:
                heapq.heappush(fut[ops[i][1]], (0.0, i))
        order = []
        LAT = 0.15
        while len(order) < n:
            best = None
            for e in engs:
                f, nw = fut[e], now[e]
                while f and f[0][0] <= etime[e]:
                    k_ = heapq.heappop(f)[1]
                    heapq.heappush(nw, (-tail[k_], k_))
                if nw:
                    cand = (etime[e], nw[0][1], e, True)
                elif f:
                    cand = (f[0][0], f[0][1], e, False)
                else:
                    continue
                if best is None or cand[:2] < best[:2]:
                    best = cand
            st, i, e, isnow = best
            if isnow:
                heapq.heappop(now[e])
            else:
                heapq.heappop(fut[e])
            o = ops[i]
            if o[0] == "dma":
                etime[e] = st + 0.06
                finish[i] = st + o[5]
            else:
                etime[e] = st + o[5]
                finish[i] = etime[e]
            order.append(i)
            for j in succs[i]:
                npred[j] -= 1
                if ready_t[j] < finish[i] + LAT:
                    ready_t[j] = finish[i] + LAT
                if npred[j] == 0:
                    heapq.heappush(fut[ops[j][1]], (ready_t[j], j))
        if self.verbose:
            busy = {e: 0.0 for e in engs}
            for o in ops:
                busy[o[1]] += (0.06 if o[0] == "dma" else o[5])
            print(f"phase {self.phase_name:14s} n={n:5d} makespan {max(finish):8.1f}us  " +
                  " ".join(f"{e}={busy[e]:6.1f}" for e in engs))
        for i in order:
            o = ops[i]
            if o[0] == "op":
                self._emit_op(o[1], o[2], o[3], o[4])
            else:
                out, in_, kw, key = o[2]
                self._emit_dma(o[1], out, in_, kw, key, o[3], o[4])

    def barrier(self, engines=("pe", "act", "dve", "sp"), name=""):
        self.phase_name = name
        self.flush()
        toks = [(self.semi[e], self.cnt[e], e) for e in ("pe", "act", "dve", "pool") if self.cnt[e] > 0]
        toks += [(ds[0], ds[1], "dma") for k, ds in self.dmasem.items() if k[0] not in self.nobarrier and ds[1] > 0]
        for eng in engines:
            self._waits(eng, [t for t in toks if t[2] != eng or eng != "pe"])

    def final_wait(self, eng="sp"):
        self.flush()
        toks = [(self.semi[e], self.cnt[e], e) for e in ("pe", "act", "dve", "pool") if self.cnt[e] > 0]
        toks += [(ds[0], ds[1], "dma") for ds in self.dmasem.values() if ds[1] > 0]
        self._waits(eng, toks)


def _snapshot(fn):
    import types
    if fn.__closure__ is None:
        return fn
    cells = []
    for cl in fn.__closure__:
        try:
            v = cl.cell_contents
        except ValueError:
            cells.append(cl)
            continue
        if isinstance(v, types.FunctionType) and v.__closure__ is not None and v is not fn:
            v = _snapshot(v)
        cells.append(types.CellType(v))
    g = types.FunctionType(fn.__code__, fn.__globals__, fn.__name__, fn.__defaults__, tuple(cells))
    g.__kwdefaults__ = fn.__kwdefaults__
    return g


def build_program(dbg_names=(), stop_after=None):
    nc = bass.Bass("TRN2", target_bir_lowering=False)
    S = Sched(nc)
    T = {}

    def din(name, shape, dt=F32):
        T[name] = nc.dram_tensor(name, list(shape), dt, kind="ExternalInput").ap()
        return T[name]

    x = din("x", [NTOK, D])
    for nm, shp in PARAM_SHAPES.items():
        din(nm, shp)
    c_ident = din("c_ident", [128, 128])
    c_tri = din("c_tri", [128, 128])
    c_pmask = din("c_pmask", [128, 4 * 128])
    out = nc.dram_tensor("out", [NTOK, D], F32, kind="ExternalOutput").ap()
    dbg = {}
    for nm, shp in dbg_names:
        dbg[nm] = nc.dram_tensor("dbg_" + nm, list(shp), F32, kind="ExternalOutput").ap()

    d_tabs = [nc.dram_tensor(f"scr_tabs{l}", [128, 32, 2, 512], BF16, kind="Internal").ap() for l in range(DEPTH)]
    d_bz = [nc.dram_tensor(f"scr_bz{l}", [128, 32, 2, 128], BF16, kind="Internal").ap() for l in range(DEPTH)]
    d_cz = [nc.dram_tensor(f"scr_cz{l}", [128, 32, 3, 128], BF16, kind="Internal").ap() for l in range(DEPTH)]

    uid = [0]

    def sb(es, name, shape, dt):
        uid[0] += 1
        return es.enter_context(nc.sbuf_tensor(f"{name}_{uid[0]}", list(shape), dt))

    def pers(name, shape, dt):
        return nc.alloc_sbuf_tensor(name, list(shape), dt)

    ident_f = pers("ident_f", [128, 128], F32)
    ident_b = pers("ident_b", [128, 128], BF16)
    tri_f = pers("tri_f", [128, 128], F32)
    ones_b = pers("ones_b", [128, 128], BF16)
    ones_f = pers("ones_f", [128, 128], F32)
    pmask = pers("pmask", [128, 4, 128], F32)
    h = pers("h", [128, 8, TT], F32)
    NWB, NWS = 4, 3
    wbuf = {"b": [pers(f"wbufb{i}", [128, 4096], BF16) for i in range(NWB)],
            "s": [pers(f"wbufs{i}", [128, 1024], BF16) for i in range(NWS)]}
    wfree = {"b": list(range(NWB)), "s": list(range(NWS))}
    gcols = {nm: pers("g_" + nm, [128, 2, 8], F32) for nm in
             ("ffn1_pre_g", "ffn1_post_g", "mix_pre_g", "mix_post_g", "ffn2_pre_g", "ffn2_post_g", "s5_d")}
    ng_col = pers("ng_col", [128, 2, 16], F32)
    cw_col = pers("cw_col", [128, 2, 4, 24], F32)
    cb_col = pers("cb_col", [128, 2, 24], F32)
    dsk_col = pers("dsk_col", [128, 2, 16], F32)
    dtb_col = pers("dtb_col", [32, 2], F32)
    aneg_col = pers("aneg_col", [32, 2], F32)
    halo = [pers(f"halo{l}", [128, 24, 3], BF16) for l in range(DEPTH)]
    S32 = [pers(f"S32_{l}", [128, 4, 512], F32) for l in range(DEPTH)]
    Sbf = [pers(f"Sbf_{l}", [128, 4, 512], BF16) for l in range(DEPTH)]
    rho = [pers(f"rho{l}", [128, 32], F32) for l in range(DEPTH)]
    r512 = [pers(f"r512_{l}", [128, 2, 32], F32) for l in range(DEPTH)]
    carry = [pers(f"carry{l}", [128, 2, 32], F32) for l in range(DEPTH)]
    rend = [pers(f"rend{l}", [128, 2, 32], F32) for l in range(DEPTH)]

    ps = [nc.alloc_psum_tensor(f"ps{i}", [128, 512], F32) for i in range(8)]
    psctr = [0]

    reserved = set()

    def PS():
        while True:
            i = psctr[0] % 8
            psctr[0] += 1
            if i not in reserved:
                return f"ps{i}", ps[i]

    def PS_hold():
        while True:
            i = psctr[0] % 8
            psctr[0] += 1
            if i not in reserved:
                reserved.add(i)
                return i, f"ps{i}", ps[i]

    def mm(o, l, r, start=True, stop=True):
        return nc.tensor.matmul(o, l, r, start=start, stop=stop)

    def dump(name, ap, reads):
        if name in dbg:
            S.dma("pool", dbg[name], ap, reads=reads, writes=["dbg_" + name])

    wctr = [0]

    wscr = {}

    def wload(src, kc, ncols, sid):
        pl = "s" if kc * ncols <= 1024 else "b"
        assert wfree[pl], "no free weight slot"
        slot = wfree[pl].pop(0)
        flat = wbuf[pl][slot][:, 0:kc * ncols]
        view = flat.rearrange("p (k f) -> p k f", k=kc)
        key = f"w{pl}{slot}"
        if sid not in wscr:
            wscr[sid] = nc.dram_tensor(f"wscr_{len(wscr)}", [128, kc * ncols], BF16, kind="Internal").ap()
            S.dma("pool", view, src, reads=[], writes=[key], persistent=True)
            S.dma("sp", wscr[sid], flat, reads=[key], writes=["wscr"], key="wscr", persistent=True)
        else:
            S.dma("sp", flat, wscr[sid], reads=["wscr"], writes=[key], key=key, persistent=True)
        return view, key

    def wrel(*keys):
        for key in keys:
            wfree[key[1]].append(int(key[2:]))

    def wslab(name, l, r0, kc, c0, ncols):
        src = T[name][l, r0:r0 + kc * 128, c0:c0 + ncols].rearrange("(k p) f -> p k f", p=128)
        return wload(src, kc, ncols, (name, l, r0, c0, ncols))

    def setup_consts():
        S.dma("sp", ident_f[:], c_ident, writes=["ident_f"])
        S.dma("sp", tri_f[:], c_tri, writes=["tri_f"])
        S.dma("sp", pmask[:], c_pmask.rearrange("p (a b) -> p a b", a=4), writes=["pmask"])
        S.op("dve", lambda: nc.vector.tensor_copy(ident_b[:], ident_f[:]), reads=["ident_f"], writes=["ident_b"])
        S.op("dve", lambda: nc.vector.memset(ones_b[:], 1.0), writes=["ones_b"])
        S.op("dve", lambda: nc.vector.memset(ones_f[:], 1.0), writes=["ones_f"])
        for nm, t in gcols.items():
            S.dma("sp", t[:], T[nm].rearrange("l (c p) -> p l c", p=128), writes=["g_" + nm],
                  allow_slow_non_contiguous=True)
        for nm in ("ffn1_post_g", "ffn2_post_g"):
            S.op("dve", lambda: nc.vector.tensor_scalar(gcols[nm][:], gcols[nm][:], 0.5, None, ALU.mult),
                 reads=[], writes=["g_" + nm])
        S.dma("sp", ng_col[:], T["ssd_norm_g"].rearrange("l (c p) -> p l c", p=128), writes=["ng_col"],
              allow_slow_non_contiguous=True)
        for l in range(DEPTH):
            for k in range(4):
                S.dma("sp", cw_col[:, l, k, :], T["ssd_conv_w"][l, k, :].rearrange("(c p) -> p c", p=128),
                      writes=["cw_col"], allow_slow_non_contiguous=True)
        S.dma("sp", cb_col[:], T["ssd_conv_b"].rearrange("l (c p) -> p l c", p=128), writes=["cb_col"],
              allow_slow_non_contiguous=True)
        dten = T["ssd_d"].tensor
        for hh in range(2):
            src = bass.AP(dten, hh, [[0, 64], [32, 2], [2, 16]])
            S.dma("sp", dsk_col[hh * 64:(hh + 1) * 64, :, :], src, writes=["dsk_col"], allow_slow_non_contiguous=True)
        S.dma("sp", dtb_col[:], T["ssd_dt_bias"].rearrange("l h -> h l"), writes=["dtb_col"],
              allow_slow_non_contiguous=True)
        S.dma("sp", aneg_col[:], T["ssd_a_log"].rearrange("l h -> h l"), writes=["aneg_col"],
              allow_slow_non_contiguous=True)
        S.op("act", lambda: nc.scalar.activation(aneg_col[:], aneg_col[:], AF.Exp), reads=[], writes=["aneg_col"])
        S.op("dve", lambda: nc.vector.tensor_scalar(aneg_col[:], aneg_col[:], -1.0, None, ALU.mult),
             writes=["aneg_col"])
        for l in range(DEPTH):
            S.op("dve", lambda: nc.vector.memset(halo[l][:], 0.0), writes=[f"halo{l}"])
            S.op("dve", lambda: nc.vector.memset(S32[l][:], 0.0), writes=[f"S32_{l}"])
            S.op("dve", lambda: nc.vector.memset(Sbf[l][:], 0.0), writes=[f"Sbf_{l}"])
            S.op("dve", lambda: nc.vector.memset(carry[l][:], 0.0), writes=[f"carry{l}"])

    def setup_s5(l, wp):
        with ExitStack() as es:
            V = lambda nm, shp, dt=F32: sb(es, nm, shp, dt)
            lre, lim, lst = V("lre", [128, 32]), V("lim", [128, 32]), V("lst", [128, 32])
            bre, bim = V("bre", [128, 32, 16]), V("bim", [128, 32, 16])
            cn = [V("cnre", [128, 8, 2, 64]), V("cnim", [128, 8, 2, 64])]
            for d in range(2):
                psl = slice(d * 64, (d + 1) * 64)
                S.dma("sp", lre[psl, :], bass.AP(T["s5_lambda_re"].tensor, l * 4096 + d * 64, [[1, 64], [128, 32]]),
                      writes=["lre"], allow_slow_non_contiguous=True)
                S.dma("sp", lim[psl, :], bass.AP(T["s5_lambda_im"].tensor, l * 4096 + d * 64, [[1, 64], [128, 32]]),
                      writes=["lim"], allow_slow_non_contiguous=True)
                S.dma("sp", lst[psl, :], bass.AP(T["s5_log_step"].tensor, l * 64 + d, [[0, 64], [2, 32]]),
                      writes=["lst"], allow_slow_non_contiguous=True)
                S.dma("sp", bre[psl, :, :], bass.AP(T["s5_b_re"].tensor, l * 65536 + d * 1024,
                                                    [[16, 64], [2048, 32], [1, 16]]), writes=["bre"])
                S.dma("sp", bim[psl, :, :], bass.AP(T["s5_b_im"].tensor, l * 65536 + d * 1024,
                                                    [[16, 64], [2048, 32], [1, 16]]), writes=["bim"])
            for ri, nm in enumerate(("s5_c_re", "s5_c_im")):
                for dup in range(2):
                    S.dma("sp", cn[ri][:, :, dup, :], bass.AP(T[nm].tensor, l * 65536, [[64, 128], [8192, 8], [1, 64]]),
                          writes=[f"cn{ri}"])
            vv = nc.vector
            tmp = [V(f"tmp{i}", [128, 32]) for i in range(8)]
            dl, are, th, cs, sn, br, bi, fre, fim = (V(n, [128, 32]) for n in
                                                     ("dl", "are", "th", "cs", "sn", "br", "bi", "fre", "fim"))
            qi = V("qi", [128, 32], I32)
            K = f"s5set{l}"
            def dve(fn):
                S.op("dve", fn, reads=["lre", "lim", "lst", "bre", "bim"], writes=[K])
            def act(fn):
                S.op("act", fn, reads=[], writes=[K])
            dve(lambda: vv.tensor_scalar(lre[:], lre[:], -1e-4, None, ALU.min))

            def horner(dst, var, coefs):
                dve(lambda: vv.tensor_scalar(dst[:], var[:], coefs[-1], 1.0, ALU.mult, ALU.add))
                for c in reversed(coefs[:-1]):
                    dve(lambda: vv.tensor_tensor(dst[:], dst[:], var[:], ALU.mult))
                    dve(lambda: vv.tensor_scalar(dst[:], dst[:], c, 1.0, ALU.mult, ALU.add))

            dve(lambda: vv.tensor_scalar(tmp[6][:], lst[:], 0.125, None, ALU.mult))
            horner(dl, tmp[6], [1.0 / k for k in range(1, 13)])
            for _ in range(3):
                dve(lambda: vv.tensor_tensor(dl[:], dl[:], dl[:], ALU.mult))
            dve(lambda: vv.tensor_tensor(are[:], lre[:], dl[:], ALU.mult))
            dve(lambda: vv.tensor_tensor(th[:], lim[:], dl[:], ALU.mult))
            em1 = V("em1", [128, 32])
            cm1 = V("cm1", [128, 32])
            horner(tmp[7], are, [1.0 / k for k in range(2, 9)])
            dve(lambda: vv.tensor_tensor(em1[:], are[:], tmp[7][:], ALU.mult))
            dve(lambda: vv.tensor_scalar(rho[l][:], em1[:], 1.0, None, ALU.add))

            def reduce_angle(dst, src, shift):
                dve(lambda: vv.tensor_scalar(tmp[0][:], src[:], shift, 1.0 / TWO_PI, ALU.add, ALU.mult))
                dve(lambda: vv.tensor_copy(qi[:], tmp[0][:]))
                dve(lambda: vv.tensor_copy(tmp[1][:], qi[:]))
                dve(lambda: vv.tensor_scalar(tmp[2][:], src[:], shift, None, ALU.add))
                dve(lambda: vv.scalar_tensor_tensor(dst[:], tmp[1][:], -TWO_PI, tmp[2][:], ALU.mult, ALU.add))
                dve(lambda: vv.tensor_scalar(tmp[3][:], dst[:], math.pi, None, ALU.is_gt))
                dve(lambda: vv.scalar_tensor_tensor(dst[:], tmp[3][:], -TWO_PI, dst[:], ALU.mult, ALU.add))
                dve(lambda: vv.tensor_scalar(tmp[3][:], dst[:], -math.pi, None, ALU.is_lt))
                dve(lambda: vv.scalar_tensor_tensor(dst[:], tmp[3][:], TWO_PI, dst[:], ALU.mult, ALU.add))

            reduce_angle(tmp[4], th, 0.0)
            xq, x2, s1, c1m = tmp[5], tmp[6], tmp[7], tmp[0]
            dve(lambda: vv.tensor_scalar(xq[:], tmp[4][:], 0.25, None, ALU.mult))
            dve(lambda: vv.tensor_tensor(x2[:], xq[:], xq[:], ALU.mult))
            horner(tmp[1], x2, [-1.0 / 6, -1.0 / 20, -1.0 / 42, -1.0 / 72, -1.0 / 110, -1.0 / 156])
            dve(lambda: vv.tensor_tensor(s1[:], xq[:], tmp[1][:], ALU.mult))
            horner(tmp[2], x2, [-1.0 / 12, -1.0 / 30, -1.0 / 56, -1.0 / 90, -1.0 / 132, -1.0 / 182])
            dve(lambda: vv.tensor_tensor(c1m[:], x2[:], tmp[2][:], ALU.mult))
            dve(lambda: vv.tensor_scalar(c1m[:], c1m[:], -0.5, None, ALU.mult))
            for _ in range(2):
                dve(lambda: vv.tensor_scalar(tmp[3][:], c1m[:], 1.0, 2.0, ALU.add, ALU.mult))
                dve(lambda: vv.tensor_tensor(tmp[1][:], s1[:], s1[:], ALU.mult))
                dve(lambda: vv.tensor_tensor(s1[:], s1[:], tmp[3][:], ALU.mult))
                dve(lambda: vv.tensor_scalar(c1m[:], tmp[1][:], -2.0, None, ALU.mult))
            dve(lambda: vv.tensor_copy(sn[:], s1[:]))
            dve(lambda: vv.tensor_copy(cm1[:], c1m[:]))
            dve(lambda: vv.tensor_scalar(cs[:], cm1[:], 1.0, None, ALU.add))
            dve(lambda: vv.tensor_tensor(br[:], em1[:], cs[:], ALU.mult))
            dve(lambda: vv.tensor_tensor(br[:], br[:], cm1[:], ALU.add))
            dve(lambda: vv.tensor_tensor(bi[:], rho[l][:], sn[:], ALU.mult))
            dve(lambda: vv.tensor_tensor(tmp[0][:], lre[:], lre[:], ALU.mult))
            dve(lambda: vv.tensor_tensor(tmp[1][:], lim[:], lim[:], ALU.mult))
            dve(lambda: vv.tensor_tensor(tmp[0][:], tmp[0][:], tmp[1][:], ALU.add))
            dve(lambda: vv.reciprocal(tmp[0][:], tmp[0][:]))
            dve(lambda: vv.tensor_copy(tmp[1][:], br[:]))
            dve(lambda: vv.tensor_tensor(tmp[2][:], tmp[1][:], lre[:], ALU.mult))
            dve(lambda: vv.tensor_tensor(tmp[3][:], bi[:], lim[:], ALU.mult))
            dve(lambda: vv.tensor_tensor(tmp[2][:], tmp[2][:], tmp[3][:], ALU.add))
            dve(lambda: vv.tensor_tensor(fre[:], tmp[2][:], tmp[0][:], ALU.mult))
            dve(lambda: vv.tensor_tensor(tmp[2][:], bi[:], lre[:], ALU.mult))
            dve(lambda: vv.tensor_tensor(tmp[3][:], tmp[1][:], lim[:], ALU.mult))
            dve(lambda: vv.tensor_tensor(tmp[2][:], tmp[2][:], tmp[3][:], ALU.subtract))
            dve(lambda: vv.tensor_tensor(fim[:], tmp[2][:], tmp[0][:], ALU.mult))
            es2 = ExitStack()
            V2 = lambda nm, shp, dt=F32: sb(es2, nm, shp, dt)
            bbr, bbi, t16a, t16b = (V2(n, [128, 32, 16]) for n in ("bbr", "bbi", "t16a", "t16b"))
            fre_b = fre[:].unsqueeze(2).broadcast_to([128, 32, 16])
            fim_b = fim[:].unsqueeze(2).broadcast_to([128, 32, 16])
            dve(lambda: vv.tensor_tensor(t16a[:], bre[:], fre_b, ALU.mult))
            dve(lambda: vv.tensor_tensor(t16b[:], bim[:], fim_b, ALU.mult))
            dve(lambda: vv.tensor_tensor(bbr[:], t16a[:], t16b[:], ALU.subtract))
            dve(lambda: vv.tensor_tensor(t16a[:], bim[:], fre_b, ALU.mult))
            dve(lambda: vv.tensor_tensor(t16b[:], bre[:], fim_b, ALU.mult))
            dve(lambda: vv.tensor_tensor(bbi[:], t16a[:], t16b[:], ALU.add))
            bex = [V2("bexr", [128, 32, 128], BF16), V2("bexi", [128, 32, 128], BF16)]
            bz_sb = V2("bz_sb", [128, 32, 2, 128], BF16)
            for ri, src in enumerate((bbr, bbi)):
                dve(lambda: vv.memset(bex[ri][:], 0.0))
                bex_v = bex[ri][:].rearrange("p (o q) c -> p o q c", q=4)
                src_v = src[:].rearrange("p (o q) c -> p o q c", q=4)
                for pp in range(4):
                    for d in range(2):
                        gl = 2 * pp + d
                        dve(lambda: vv.tensor_copy(bex_v[d * 64:(d + 1) * 64, :, pp, gl * 16:(gl + 1) * 16],
                                                   src_v[d * 64:(d + 1) * 64, :, pp, :]))
            for ri in range(2):
                for pb in range(8):
                    pk, pt = PS()
                    S.op("pe", lambda: [mm(pt[:, j * 128:(j + 1) * 128], bex[ri][:, pb * 4 + j, :], ident_b[:])
                                        for j in range(4)], reads=[K, "ident_b"], writes=[pk])
                    S.op("act", lambda: nc.scalar.copy(bz_sb[:, pb * 4:(pb + 1) * 4, ri, :],
                                                       pt[:].rearrange("p (j m) -> p j m", j=4)),
                         reads=[pk], writes=["bz_sb"])
            S.dma("sp", d_bz[l], bz_sb[:], reads=["bz_sb"], writes=[f"d_bz{l}"])
            cz_sb = V2("cz_sb", [128, 32, 3, 128], BF16)
            for ri in range(2):
                for o in range(8):
                    pk, pt = PS()
                    S.op("pe", lambda: nc.tensor.transpose(pt[:, 0:128],
                                                           cn[ri][:, o, :, :].rearrange("p a n -> p (a n)"),
                                                           ident_f[:]),
                         reads=[f"cn{ri}", "ident_f"], writes=[pk])
                    for var, sgn in (((0, 1.0), (1, -1.0)) if ri == 0 else ((2, -1.0),)):
                        for pp in range(4):
                            S.op("dve", lambda: vv.scalar_tensor_tensor(cz_sb[:, o * 4 + pp, var, :], pt[:, 0:128], sgn,
                                                                        pmask[:, pp, :], ALU.mult, ALU.mult),
                                 reads=[pk, "pmask"], writes=["cz_sb"])
            S.dma("sp", d_cz[l], cz_sb[:], reads=["cz_sb"], writes=[f"d_cz{l}"])
            S.barrier()
            es2.close()
            dve(lambda: vv.tensor_copy(wp[:, 0, 0, :], cs[:]))
            dve(lambda: vv.tensor_copy(wp[:, 0, 1, :], sn[:]))
            for m in range(9):
                dve(lambda: vv.tensor_tensor(tmp[0][:], wp[:, m, 0, :], wp[:, m, 0, :], ALU.mult))
                dve(lambda: vv.tensor_tensor(tmp[1][:], wp[:, m, 1, :], wp[:, m, 1, :], ALU.mult))
                dve(lambda: vv.tensor_tensor(wp[:, m + 1, 0, :], tmp[0][:], tmp[1][:], ALU.subtract))
                dve(lambda: vv.tensor_tensor(tmp[2][:], wp[:, m, 0, :], wp[:, m, 1, :], ALU.mult))
                dve(lambda: vv.tensor_scalar(wp[:, m + 1, 1, :], tmp[2][:], 2.0, None, ALU.mult))
            dve(lambda: vv.tensor_copy(r512[l][:], wp[:, 9, :, :]))
            S.barrier()

    def setup_tables(wps):
        with ExitStack() as es:
            for si, (eng, items) in enumerate((("dve", ((0, 0), (0, 1), (0, 2), (0, 3), (1, 0))),
                                               ("pool", ((1, 1), (1, 2), (1, 3))))):
                ee_ = nc.vector if eng == "dve" else nc.gpsimd
                K = f"tabs{si}"
                tab = sb(es, f"tab{si}", [128, 8, 2, 512], F32)
                ta, tb_ = sb(es, f"ta{si}", [128, 8, 256], F32), sb(es, f"tb{si}", [128, 8, 256], F32)
                for (l, b) in items:
                    wp = wps[l]

                    def do(fn, c=None):
                        S.op(eng, fn, reads=[f"s5set{l}"], writes=[K], c=(c if eng == "dve" else 2 * c))
                    do(lambda: ee_.memset(tab[:, :, 0, 0:1], 1.0), 0.1)
                    do(lambda: ee_.memset(tab[:, :, 1, 0:1], 0.0), 0.1)
                    for m in range(9):
                        n = 1 << m
                        wr = wp[:, m, 0, b * 8:(b + 1) * 8].unsqueeze(2).broadcast_to([128, 8, n])
                        wi = wp[:, m, 1, b * 8:(b + 1) * 8].unsqueeze(2).broadcast_to([128, 8, n])
                        tr0, ti0 = tab[:, :, 0, 0:n], tab[:, :, 1, 0:n]
                        cst = 0.15 + 8 * n * 0.0022
                        do(lambda: ee_.tensor_tensor(ta[:, :, 0:n], tr0, wr, ALU.mult), cst)
                        do(lambda: ee_.tensor_tensor(tb_[:, :, 0:n], ti0, wi, ALU.mult), cst)
                        do(lambda: ee_.tensor_tensor(tab[:, :, 0, n:2 * n], ta[:, :, 0:n], tb_[:, :, 0:n], ALU.subtract), cst)
                        do(lambda: ee_.tensor_tensor(ta[:, :, 0:n], tr0, wi, ALU.mult), cst)
                        do(lambda: ee_.tensor_tensor(tb_[:, :, 0:n], ti0, wr, ALU.mult), cst)
                        do(lambda: ee_.tensor_tensor(tab[:, :, 1, n:2 * n], ta[:, :, 0:n], tb_[:, :, 0:n], ALU.add), cst)
                    S.dma("pool", d_tabs[l][:, b * 8:(b + 1) * 8, :, :], tab[:], reads=[K], writes=[K, f"d_tabs{l}_{b}"],
                          key=f"d_tabs{si}")
            S.barrier()

    def load_tile(tau):
        with ExitStack() as es:
            stage = sb(es, "stage", [128, 4, D], F32)
            for tc in range(4):
                r0 = tau * TT + tc * 128
                S.dma("sp", stage[:, tc, :], x[r0:r0 + 128, :], writes=[f"stage{tc}"])
            for c in range(8):
                pk, pt = PS()
                S.op("pe", lambda: [nc.tensor.transpose(pt[:, tc * 128:(tc + 1) * 128],
                                                        stage[:, tc, c * 128:(c + 1) * 128], ident_f[:])
                                    for tc in range(4)],
                     reads=[f"stage{tc}" for tc in range(4)] + ["ident_f"], writes=[pk])
                S.op("act", lambda: nc.scalar.copy(h[:, c, :], pt[:]), reads=[pk], writes=[f"h{c}"])
            S.barrier()

    def store_tile(tau):
        with ExitStack() as es:
            stage = sb(es, "stage", [128, 4, D], F32)
            for tc in range(4):
                for half in range(2):
                    pk, pt = PS()
                    S.op("pe", lambda: [nc.tensor.transpose(pt[:, j * 128:(j + 1) * 128],
                                                            h[:, half * 4 + j, tc * 128:(tc + 1) * 128], ident_f[:])
                                        for j in range(4)],
                         reads=[f"h{half * 4 + j}" for j in range(4)] + ["ident_f"], writes=[pk])
                    S.op("act", lambda: nc.scalar.copy(stage[:, tc, half * 512:(half + 1) * 512], pt[:]),
                         reads=[pk], writes=[f"stage{tc}"])
                r0 = tau * TT + tc * 128
                S.dma("sp", out[r0:r0 + 128, :], stage[:, tc, :], reads=[f"stage{tc}"], writes=[f"out{tc}"])
            S.barrier()

    def rstd_from_sq(sq, nch, denom, rs, sqkeys):
        pk, pt = PS()
        S.op("pe", lambda: [mm(pt[:], ones_b[:], sq[:, c, :], start=(c == 0), stop=(c == nch - 1))
                            for c in range(nch)], reads=list(sqkeys) + ["ones_b"], writes=[pk])
        S.op("act", lambda: nc.scalar.activation(rs[:], pt[:], AF.Sqrt, bias=eps_col[:], scale=1.0 / denom),
             reads=[pk], writes=["rs"])
        S.op("dve", lambda: nc.vector.reciprocal(rs[:], rs[:]), reads=[], writes=["rs"])

    def norm_pre(xn, sq, rs, gname, l):
        g = gcols[gname]
        for c in range(8):
            S.op("act", lambda: nc.scalar.activation(sq[:, c, :], h[:, c, :], AF.Square),
                 reads=[f"h{c}"], writes=[f"sq{c}"])
        rstd_from_sq(sq, 8, float(D), rs, [f"sq{c}" for c in range(8)])
        for c in range(8):
            S.op("dve", lambda: nc.vector.scalar_tensor_tensor(xn[:, c, :], h[:, c, :], g[:, l, c:c + 1], rs[:],
                                                               ALU.mult, ALU.mult),
                 reads=[f"h{c}", "rs", "g_" + gname], writes=[f"xn{c}"])

    def post_norm_residual(f, sq, rs, gname, l, tmp):
        g = gcols[gname]
        rstd_from_sq(sq, 8, float(D), rs, [f"sq{c}" for c in range(8)])
        for c in range(8):
            S.op("dve", lambda: nc.vector.scalar_tensor_tensor(tmp[:], f[:, c, :], g[:, l, c:c + 1], rs[:],
                                                               ALU.mult, ALU.mult),
                 reads=[f"f{c}", "rs", "g_" + gname], writes=["pn_tmp"])
            S.op("dve", lambda: nc.vector.tensor_tensor(h[:, c, :], h[:, c, :], tmp[:], ALU.add),
                 reads=["pn_tmp"], writes=[f"h{c}"])

    def proj_out(f, sq, wname, l, src, srckeys, kc):
        for ds in range(4):
            wv, wk = wslab(wname, l, 0, kc, ds * 256, 256) if kc <= 16 else (None, None)
            for j in range(2):
                dc = 2 * ds + j
                pk, pt = PS()
                S.op("pe", lambda: [mm(pt[:], wv[:, k, j * 128:(j + 1) * 128], src[:, k, :], start=(k == 0),
                                       stop=(k == kc - 1)) for k in range(kc)],
                     reads=[wk] + list(srckeys), writes=[pk], c=2.2)
                S.op("act", lambda: nc.scalar.copy(f[:, dc, :], pt[:]), reads=[pk], writes=[f"f{dc}"])
                S.op("act", lambda: nc.scalar.activation(sq[:, dc, :], pt[:], AF.Square), reads=[pk],
                     writes=[f"sq{dc}"])
            wrel(wk)

    def ffn(l, which):
        pre = f"ffn{which}"
        with ExitStack() as es:
            xn = sb(es, "xn", [128, 8, TT], BF16)
            sq = sb(es, "sq", [128, 8, TT], BF16)
            rs = sb(es, "rs", [128, TT], F32)
            hb = sb(es, "hb", [128, FC, TT], BF16)
            sg = [sb(es, f"sg{i}", [128, TT], BF16) for i in range(2)]
            f = sb(es, "f", [128, 8, TT], F32)
            tmp = sb(es, "pntmp", [128, TT], F32)
            norm_pre(xn, sq, rs, pre + "_pre_g", l)
            xk = [f"xn{c}" for c in range(8)]
            for fs in range(FC // 2):
                wg, wgk = wslab(pre + "_w_gate", l, 0, 8, fs * 256, 256)
                wu, wuk = wslab(pre + "_w_up", l, 0, 8, fs * 256, 256)
                for j in range(2):
                    fc = 2 * fs + j
                    pka, pa = PS()
                    pkb, pb = PS()
                    S.op("pe", lambda: [mm(pa[:], wg[:, k, j * 128:(j + 1) * 128], xn[:, k, :], start=(k == 0),
                                           stop=(k == 7)) for k in range(8)], reads=[wgk] + xk, writes=[pka], c=2.2)
                    S.op("pe", lambda: [mm(pb[:], wu[:, k, j * 128:(j + 1) * 128], xn[:, k, :], start=(k == 0),
                                           stop=(k == 7)) for k in range(8)], reads=[wuk] + xk, writes=[pkb], c=2.2)
                    sgt = sg[fc % 2]
                    S.op("act", lambda: nc.scalar.activation(sgt[:], pa[:], AF.Silu), reads=[pka],
                         writes=[f"sg{fc % 2}"])
                    S.op("dve", lambda: nc.vector.tensor_tensor(hb[:, fc, :], sgt[:], pb[:], ALU.mult),
                         reads=[f"sg{fc % 2}", pkb], writes=[f"hb{fc}"])
                wrel(wgk, wuk)
            hk = [f"hb{c}" for c in range(FC)]
            for dc in range(8):
                src = T[pre + "_w_down"][l, :, dc * 128:(dc + 1) * 128].rearrange("(k p) f -> p k f", p=128)
                wv, wk = wload(src, FC, 128, (pre + "_w_down", l, dc))
                pk, pt = PS()
                S.op("pe", lambda: [mm(pt[:], wv[:, k, :], hb[:, k, :], start=(k == 0), stop=(k == FC - 1))
                                    for k in range(FC)], reads=[wk] + hk, writes=[pk], c=5.9)
                S.op("act", lambda: nc.scalar.copy(f[:, dc, :], pt[:]), reads=[pk], writes=[f"f{dc}"])
                S.op("act", lambda: nc.scalar.activation(sq[:, dc, :], pt[:], AF.Square), reads=[pk],
                     writes=[f"sq{dc}"])
                wrel(wk)
            post_norm_residual(f, sq, rs, pre + "_post_g", l, tmp)
            S.barrier()

    def mixer(l, tau):
        vv = nc.vector
        with ExitStack() as es_m:
            xn = sb(es_m, "xn", [128, 8, TT], BF16)
            xk = [f"xn{c}" for c in range(8)]
            gl_ = sb(es_m, "gl", [128, 8, TT], BF16)
            YN = sb(es_m, "YN", [128, 16, TT], BF16)
            es_a = ExitStack()
            es_b = ExitStack()
            VA = lambda nm, shp, dt=F32: sb(es_a, nm, shp, dt)
            VB = lambda nm, shp, dt=F32: sb(es_b, nm, shp, dt)
            sq = VA("sq4", [128, 4, TT], BF16)
            rs = VA("rs", [128, TT])
            dtT_full = VA("dtT", [128, TT])
            dtT, adtT = dtT_full[0:32, :], VA("adtT", [32, TT])[:]
            dt_tok, adt_tok, acum_tok = VA("dt_tok", [128, 4, 32]), VA("adt_tok", [128, 4, 32]), VA("acum_tok", [128, 4, 32])
            cd, dstate, ea, dtd = VA("cd", [128, 4, 32]), VA("dstate", [128, 4, 32]), VA("ea", [128, 4, 32]), VA("dtd", [128, 4, 32])
            raw = VA("raw", [128, 6, TT + 3], BF16)
            xc = VA("xc", [128, 6, TT], BF16)
            X_tok, Xd = VA("X_tok", [128, 4, 512], BF16), VA("Xd", [128, 512], BF16)
            B_tok = VA("B_tok", [128, 4, 128], BF16)
            cbm = VA("cbm", [128, 128], BF16)
            dd = VA("dd", [128, 8, 128], BF16)
            ee, mmx = VA("ee", [128, 8, 128], BF16), VA("mmx", [128, 8, 128], BF16)
            yo = VA("yo", [128, 512], BF16)
            yT = VA("yT", [128, 4, TT])
            zs = VA("zs", [128, TT], BF16)
            dg = [VA(f"dg{k}", [128, 128], BF16) for k in range(4)]
            u5 = VB("u5", [128, 8, TT], BF16)
            tb = [VB(f"tbl{i}", [128, 2, 512], BF16) for i in range(2)]
            bzo = [VB(f"bzo{i}", [128, 4, 2, 128], BF16) for i in range(2)]
            czo = [VB(f"czo{i}", [128, 4, 3, 128], BF16) for i in range(2)]
            pA, pB = VB("pA", [128, 2, TT], BF16), VB("pB", [128, 2, TT], BF16)
            QQ = [(VB(f"qA{i}", [128, 2, TT], BF16), VB(f"qB{i}", [128, 2, TT], BF16)) for i in range(2)]
            Wr, Wi = VB("Wr", [128, TT], BF16), VB("Wi", [128, TT], BF16)
            Rr, Ri = VB("Rr", [128, TT], BF16), VB("Ri", [128, TT], BF16)
            y5 = VB("y5", [128, TT])
            g1 = dtT_full

            norm_pre(xn, YN, rs, "mix_pre_g", l)
            wv, wk = wslab("w_in", l, 0, 8, OFF_DT, 32)
            pk, pt = PS()
            S.op("pe", lambda: [mm(pt[0:32, :], wv[:, k, :], xn[:, k, :], start=(k == 0), stop=(k == 7))
                                for k in range(8)], reads=[wk] + xk, writes=[pk], c=2.2)
            wrel(wk)
            S.op("act", lambda: nc.scalar.activation(dtT, pt[0:32, :], AF.Exp, bias=dtb_col[:, l:l + 1]),
                 reads=[pk, "dtb_col"], writes=["dtT"])
            S.op("act", lambda: nc.scalar.activation(dtT, dtT, AF.Ln, bias=one_col[0:32, :]),
                 reads=[], writes=["dtT"])
            S.op("dve", lambda: nc.vector.tensor_scalar(adtT, dtT, aneg_col[:, l:l + 1], None, ALU.mult),
                 reads=["dtT", "aneg_col"], writes=["adtT"])
            pk, pt = PS()
            S.op("pe", lambda: [mm(pt[:, ch * 32:(ch + 1) * 32], dtT_full[0:32, ch * 128:(ch + 1) * 128],
                                   ident_f[0:32, 0:32]) for ch in range(4)] +
                               [mm(pt[:, 128 + ch * 32:128 + (ch + 1) * 32], adtT[:, ch * 128:(ch + 1) * 128],
                                   ident_f[0:32, 0:32]) for ch in range(4)],
                 reads=["dtT", "adtT", "ident_f"], writes=[pk])
            S.op("act", lambda: nc.scalar.copy(dt_tok[:].rearrange("p a b -> p (a b)"), pt[:, 0:128]),
                 reads=[pk], writes=["dt_tok"])
            S.op("act", lambda: nc.scalar.copy(adt_tok[:].rearrange("p a b -> p (a b)"), pt[:, 128:256]),
                 reads=[pk], writes=["adt_tok"])
            pk, pt = PS()
            S.op("pe", lambda: [mm(pt[:, ch * 32:(ch + 1) * 32], tri_f[:], adt_tok[:, ch, :]) for ch in range(4)] +
                               [mm(pt[:, 128 + ch * 32:128 + (ch + 1) * 32], ones_f[:], adt_tok[:, ch, :])
                                for ch in range(4)],
                 reads=["adt_tok", "tri_f", "ones_f"], writes=[pk])
            fl = lambda t: t[:].rearrange("p a b -> p (a b)")
            S.op("act", lambda: nc.scalar.copy(fl(acum_tok), pt[:, 0:128]), reads=[pk], writes=["acum_tok"])
            S.op("act", lambda: nc.scalar.activation(fl(cd), pt[:, 128:256], AF.Exp), reads=[pk], writes=["cd"])
            S.op("dve", lambda: nc.vector.tensor_tensor(fl(dstate), pt[:, 128:256], fl(acum_tok), ALU.subtract),
                 reads=[pk, "acum_tok"], writes=["dstate"])
            S.op("act", lambda: nc.scalar.activation(fl(dstate), fl(dstate), AF.Exp), reads=[], writes=["dstate"])
            S.op("act", lambda: nc.scalar.activation(fl(ea), fl(acum_tok), AF.Exp), reads=["acum_tok"],
                 writes=["ea"])
            S.op("dve", lambda: nc.vector.tensor_tensor(fl(dtd), fl(dt_tok), fl(dstate), ALU.mult),
                 reads=["dt_tok", "dstate"], writes=["dtd"])

            def ssd_gen():
                for g in range(4):
                    chans = [4 * g + i for i in range(4)] + [16 + g, 20 + g]
                    wx, wxk = wslab("w_in", l, 0, 8, OFF_X + 512 * g, 512)
                    wb, wbk = wslab("w_in", l, 0, 8, OFF_B + 128 * g, 128)
                    wc, wck = wslab("w_in", l, 0, 8, OFF_C + 128 * g, 128)
                    for cc in range(6):
                        if cc < 4:
                            wsel, wkk, c0 = wx, wxk, cc * 128
                        elif cc == 4:
                            wsel, wkk, c0 = wb, wbk, 0
                        else:
                            wsel, wkk, c0 = wc, wck, 0
                        ch24 = chans[cc]
                        pk, pt = PS()
                        S.op("pe", lambda: [mm(pt[:], wsel[:, k, c0:c0 + 128], xn[:, k, :], start=(k == 0),
                                               stop=(k == 7)) for k in range(8)], reads=[wkk] + xk, writes=[pk], c=2.2)
                        S.op("act", lambda: nc.scalar.copy(raw[:, cc, 0:3], halo[l][:, ch24, :]),
                             reads=[f"halo{l}_{ch24}"], writes=[f"raw{cc}"], c=0.2)
                        S.op("act", lambda: nc.scalar.copy(raw[:, cc, 3:TT + 3], pt[:]), reads=[pk],
                             writes=[f"raw{cc}"])
                        S.op("act", lambda: nc.scalar.copy(halo[l][:, ch24, :], raw[:, cc, TT:TT + 3]),
                             reads=[f"raw{cc}"], writes=[f"halo{l}_{ch24}"], c=0.2)
                        for k in range(4):
                            S.op("act", lambda: nc.scalar.activation(dg[k][:], ident_b[:], AF.Copy,
                                                                     scale=cw_col[:, l, k, ch24:ch24 + 1]),
                                 reads=["ident_b", "cw_col"], writes=[f"dg{k}"], c=0.25)
                        pk2, pt2 = PS()
                        S.op("pe", lambda: [mm(pt2[:], dg[k][:], raw[:, cc, k:k + TT], start=(k == 0), stop=(k == 3))
                                            for k in range(4)], reads=[f"dg{k}" for k in range(4)] + [f"raw{cc}"],
                             writes=[pk2], c=1.1)
                        S.op("act", lambda: nc.scalar.activation(xc[:, cc, :], pt2[:], AF.Silu,
                                                                 bias=cb_col[:, l, ch24:ch24 + 1]),
                             reads=[pk2, "cb_col"], writes=[f"xc{cc}"])
                        yield
                    wrel(wxk, wbk, wck)
                    for ch in range(4):
                        pk, pt = PS()
                        S.op("pe", lambda: [mm(pt[:, i * 128:(i + 1) * 128], xc[:, i, ch * 128:(ch + 1) * 128],
                                               ident_b[:]) for i in range(4)],
                             reads=[f"xc{i}" for i in range(4)] + ["ident_b"], writes=[pk])
                        ptv = pt[:].rearrange("p (a b) -> p a b", a=8)
                        dtb = dt_tok[:, ch, 8 * g:8 * g + 8].unsqueeze(2).broadcast_to([128, 8, 64])
                        S.op("dve", lambda: nc.vector.tensor_tensor(X_tok[:, ch, :].rearrange("p (a b) -> p a b", a=8),
                                                                    ptv, dtb, ALU.mult),
                             reads=[pk, "dt_tok"], writes=[f"X_tok{ch}"])
                    pk, pt = PS()
                    S.op("pe", lambda: [mm(pt[:, ch * 128:(ch + 1) * 128], xc[:, 4, ch * 128:(ch + 1) * 128],
                                           ident_b[:]) for ch in range(4)], reads=["xc4", "ident_b"], writes=[pk])
                    S.op("act", lambda: nc.scalar.copy(B_tok[:].rearrange("p a b -> p (a b)"), pt[:]), reads=[pk],
                         writes=["B_tok"])
                    yield
                    for ch in range(4):
                        csl = slice(ch * 128, (ch + 1) * 128)
                        pk, pt = PS()
                        S.op("pe", lambda: mm(pt[:, 0:128], xc[:, 4, csl], xc[:, 5, csl]), reads=["xc4", "xc5"],
                             writes=[pk])
                        S.op("dve", lambda: nc.vector.tensor_tensor(cbm[:], pt[:, 0:128], tri_f[:], ALU.mult),
                             reads=[pk, "tri_f"], writes=["cbm"])
                        pka, pa = PS()
                        pkb, pb = PS()
                        S.op("pe", lambda: [mm((pa if hh < 4 else pb)[:, (hh % 4) * 128:(hh % 4 + 1) * 128],
                                               adt_tok[:, ch, 8 * g + hh:8 * g + hh + 1].broadcast_to([128, 128]),
                                               tri_f[:]) for hh in range(8)],
                             reads=["adt_tok", "tri_f"], writes=[pka, pkb])
                        for hh in range(8):
                            src = (pa if hh < 4 else pb)[:, (hh % 4) * 128:(hh % 4 + 1) * 128]
                            S.op("act", lambda: nc.scalar.activation(dd[:, hh, :], src, AF.Relu,
                                                                     bias=acum_tok[:, ch, 8 * g + hh:8 * g + hh + 1],
                                                                     scale=-1.0),
                                 reads=[pka, pkb, "acum_tok"], writes=["dd"])
                        S.op("act", lambda: nc.scalar.activation(ee[:], dd[:], AF.Exp, scale=-1.0), reads=["dd"],
                             writes=["ee"])
                        S.op("pool", lambda: nc.gpsimd.tensor_tensor(mmx[:], ee[:],
                                                                     cbm[:].unsqueeze(1).broadcast_to([128, 8, 128]),
                                                                     ALU.mult), reads=["ee", "cbm"], writes=["mmx"], c=2.4)
                        pk, pt = PS()
                        S.op("pe", lambda: mm(pt[:], xc[:, 5, csl], Sbf[l][:, g, :]), reads=["xc5", f"Sbf{l}_{g}"],
                             writes=[pk])
                        S.op("dve", lambda: nc.vector.tensor_tensor(
                            yo[:].rearrange("p (a b) -> p a b", a=8), pt[:].rearrange("p (a b) -> p a b", a=8),
                            ea[:, ch, 8 * g:8 * g + 8].unsqueeze(2).broadcast_to([128, 8, 64]), ALU.mult),
                             reads=[pk, "ea"], writes=["yo"])
                        pky, py = PS()

                        def ymm():
                            r = []
                            for hp in range(4):
                                cs_ = slice(hp * 128, (hp + 1) * 128)
                                r.append(mm(py[:, cs_], yo[:, cs_], ident_b[:], start=True, stop=False))
                                r.append(mm(py[0:64, cs_], X_tok[:, ch, (2 * hp) * 64:(2 * hp + 1) * 64],
                                            mmx[:, 2 * hp, :], start=False, stop=True))
                                r.append(mm(py[64:128, cs_], X_tok[:, ch, (2 * hp + 1) * 64:(2 * hp + 2) * 64],
                                            mmx[:, 2 * hp + 1, :], start=False, stop=True))
                            return r
                        S.op("pe", ymm, reads=["yo", "ident_b", f"X_tok{ch}", "mmx"], writes=[pky])
                        for hp in range(4):
                            S.op("dve", lambda: nc.vector.scalar_tensor_tensor(
                                yT[:, hp, csl], xc[:, hp, csl], dsk_col[:, l, 4 * g + hp:4 * g + hp + 1],
                                py[:, hp * 128:(hp + 1) * 128], ALU.mult, ALU.add),
                                 reads=[f"xc{hp}", "dsk_col", pky], writes=[f"yT{hp}"])
                        S.op("pool", lambda: nc.gpsimd.tensor_tensor(
                            Xd[:].rearrange("p (a b) -> p a b", a=8), X_tok[:, ch, :].rearrange("p (a b) -> p a b", a=8),
                            dstate[:, ch, 8 * g:8 * g + 8].unsqueeze(2).broadcast_to([128, 8, 64]), ALU.mult),
                             reads=[f"X_tok{ch}", "dstate"], writes=["Xd"], c=1.2)
                        pk, pt = PS()
                        S.op("pe", lambda: mm(pt[:], B_tok[:, ch, :], Xd[:]), reads=["B_tok", "Xd"],
                             writes=[pk])
                        s3 = S32[l][:, g, :].rearrange("p (a b) -> p a b", a=8)
                        S.op("dve", lambda: nc.vector.tensor_tensor(
                            s3, s3, cd[:, ch, 8 * g:8 * g + 8].unsqueeze(2).broadcast_to([128, 8, 64]), ALU.mult),
                             reads=["cd"], writes=[f"S32{l}_{g}"])
                        S.op("dve", lambda: nc.vector.tensor_tensor(S32[l][:, g, :], S32[l][:, g, :], pt[:], ALU.add),
                             reads=[pk], writes=[f"S32{l}_{g}"])
                        S.op("act", lambda: nc.scalar.copy(Sbf[l][:, g, :], S32[l][:, g, :]), reads=[f"S32{l}_{g}"],
                             writes=[f"Sbf{l}_{g}"])
                        yield
                    wz, wzk = wslab("w_in", l, 0, 8, OFF_Z + 512 * g, 512)
                    for zc in range(4):
                        pk, pt = PS()
                        S.op("pe", lambda: [mm(pt[:], wz[:, k, zc * 128:(zc + 1) * 128], xn[:, k, :], start=(k == 0),
                                               stop=(k == 7)) for k in range(8)], reads=[wzk] + xk, writes=[pk], c=2.2)
                        S.op("act", lambda: nc.scalar.activation(zs[:], pt[:], AF.Silu), reads=[pk], writes=["zs"])
                        S.op("pool", lambda: nc.gpsimd.tensor_tensor(yT[:, zc, :], yT[:, zc, :], zs[:], ALU.mult),
                             reads=["zs"], writes=[f"yT{zc}"], c=1.2)
                        S.op("act", lambda: nc.scalar.activation(sq[:, zc, :], yT[:, zc, :], AF.Square),
                             reads=[f"yT{zc}"], writes=[f"sq{zc}"])
                    wrel(wzk)
                    rstd_from_sq(sq, 4, 512.0, rs, [f"sq{c}" for c in range(4)])
                    for zc in range(4):
                        S.op("dve", lambda: nc.vector.scalar_tensor_tensor(
                            YN[:, 4 * g + zc, :], yT[:, zc, :], ng_col[:, l, 4 * g + zc:4 * g + zc + 1], rs[:],
                            ALU.mult, ALU.mult), reads=[f"yT{zc}", "ng_col", "rs"], writes=[f"YN{4 * g + zc}"])
                    yield

            def s5_gen():
                for half in range(2):
                    wv, wk = wslab("w_in", l, 0, 8, OFF_U5 + 512 * half, 512)
                    for j in range(4):
                        c = half * 4 + j
                        pk, pt = PS()
                        S.op("pe", lambda: [mm(pt[:], wv[:, k, j * 128:(j + 1) * 128], xn[:, k, :], start=(k == 0),
                                               stop=(k == 7)) for k in range(8)], reads=[wk] + xk, writes=[pk], c=2.2)
                        S.op("act", lambda: nc.scalar.copy(u5[:, c, :], pt[:]), reads=[pk], writes=[f"u5{c}"])
                    wrel(wk)
                    yield
                for o in range(8):
                    ob = o % 2
                    S.dma("sp", bzo[ob][:], d_bz[l][:, 4 * o:4 * o + 4, :, :], reads=[f"d_bz{l}"], writes=[f"bzo{ob}"])
                    S.dma("sp", czo[ob][:], d_cz[l][:, 4 * o:4 * o + 4, :, :], reads=[f"d_cz{l}"], writes=[f"czo{ob}"])
                    pyi, pky, py = PS_hold()
                    for pp in range(4):
                        P = 4 * o + pp
                        tbi = P % 2
                        tbl = tb[tbi]
                        S.dma("sp", tbl[:], d_tabs[l][:, P, :, :], reads=[f"d_tabs{l}_{P // 8}"], writes=[f"tbl{tbi}"])
                        pkr, pr = PS()
                        pki, pi_ = PS()
                        S.op("pe", lambda: mm(pr[:], bzo[ob][:, pp, 0, :], u5[:, o, :]), reads=[f"bzo{ob}", f"u5{o}"],
                             writes=[pkr])
                        S.op("pe", lambda: mm(pi_[:], bzo[ob][:, pp, 1, :], u5[:, o, :]), reads=[f"bzo{ob}", f"u5{o}"],
                             writes=[pki])
                        tk = f"tbl{tbi}"
                        cT, sT = tbl[:, 0, :], tbl[:, 1, :]
                        bc2 = lambda ap: ap.unsqueeze(1).broadcast_to([128, 2, TT])
                        S.op("dve", lambda: vv.tensor_tensor(pA[:], bc2(pr[:]), tbl[:], ALU.mult), reads=[pkr, tk],
                             writes=["pA"], c=1.15)
                        S.op("dve", lambda: vv.tensor_tensor(pB[:], bc2(pi_[:]), tbl[:], ALU.mult), reads=[pki, tk],
                             writes=["pB"], c=1.15)
                        S.op("pool", lambda: nc.gpsimd.tensor_tensor(Wr[:], pA[:, 0, :], pB[:, 1, :], ALU.add),
                             reads=["pA", "pB"], writes=["Wr"])
                        S.op("pool", lambda: nc.gpsimd.tensor_tensor(Wi[:], pB[:, 0, :], pA[:, 1, :], ALU.subtract),
                             reads=["pA", "pB"], writes=["Wi"])
                        rb = rho[l][:, P:P + 1].broadcast_to([128, TT])
                        S.op("dve", lambda: vv.tensor_tensor_scan(Rr[:], rb, Wr[:], carry[l][:, 0, P:P + 1], ALU.mult,
                                                                  ALU.add), reads=["Wr", f"carry{l}"], writes=["Rr"], c=1.15)
                        S.op("dve", lambda: vv.tensor_tensor_scan(Ri[:], rb, Wi[:], carry[l][:, 1, P:P + 1], ALU.mult,
                                                                  ALU.add), reads=["Wi", f"carry{l}"], writes=["Ri"], c=1.15)
                        S.op("act", lambda: nc.scalar.copy(rend[l][:, 0, P:P + 1], Rr[:, TT - 1:TT]), reads=["Rr"],
                             writes=[f"rend{l}"], c=0.2)
                        S.op("act", lambda: nc.scalar.copy(rend[l][:, 1, P:P + 1], Ri[:, TT - 1:TT]), reads=["Ri"],
                             writes=[f"rend{l}"], c=0.2)
                        qA, qB = QQ[tbi]
                        qk = f"q{tbi}"
                        S.op("dve", lambda: vv.tensor_tensor(qA[:], bc2(Rr[:]), tbl[:], ALU.mult), reads=["Rr", tk],
                             writes=[qk + "A"], c=1.15)
                        S.op("dve", lambda: vv.tensor_tensor(qB[:], bc2(Ri[:]), tbl[:], ALU.mult), reads=["Ri", tk],
                             writes=[qk + "B"], c=1.15)
                        S.op("pe", lambda: [mm(py[:], czo[ob][:, pp, 0, :], qA[:, 0, :], start=(pp == 0), stop=False),
                                            mm(py[:], czo[ob][:, pp, 1, :], qB[:, 1, :], start=False, stop=False),
                                            mm(py[:], czo[ob][:, pp, 2, :], qA[:, 1, :], start=False, stop=False),
                                            mm(py[:], czo[ob][:, pp, 2, :], qB[:, 0, :], start=False, stop=(pp == 3))],
                             reads=[f"czo{ob}", qk + "A", qk + "B"], writes=[pky], c=1.1)
                        yield
                    S.op("dve", lambda: vv.scalar_tensor_tensor(y5[:], u5[:, o, :], gcols["s5_d"][:, l, o:o + 1], py[:],
                                                                ALU.mult, ALU.add), reads=[f"u5{o}", "g_s5_d", pky],
                         writes=["y5"])
                    reserved.discard(pyi)
                    S.op("dve", lambda: vv.tensor_tensor(g1[:], y5[:], y5[:], ALU.mult), reads=["y5"], writes=["dtT"])
                    S.op("dve", lambda: vv.tensor_scalar(g1[:], g1[:], 0.044715, 1.0, ALU.mult, ALU.add), reads=[],
                         writes=["dtT"])
                    S.op("dve", lambda: vv.tensor_tensor(g1[:], g1[:], y5[:], ALU.mult), reads=["y5"], writes=["dtT"])
                    S.op("act", lambda: nc.scalar.activation(g1[:], g1[:], AF.Sigmoid, scale=1.5957691216057308),
                         reads=[], writes=["dtT"])
                    S.op("dve", lambda: vv.tensor_tensor(gl_[:, o, :], y5[:], g1[:], ALU.mult), reads=["y5", "dtT"],
                         writes=[f"gl{o}"])
                    yield
                a, b_ = r512[l], rend[l]
                t1, t2 = p1[:, 0:32].bitcast(BF16) if False else None, None
                ck = [f"rend{l}", f"carry{l}"]
                c1t, c2t = y5[:, 0:32], y5[:, 32:64]
                S.op("dve", lambda: vv.tensor_tensor(c1t, a[:, 0, :], b_[:, 0, :], ALU.mult), reads=ck + ["y5"], writes=["y5"])
                S.op("dve", lambda: vv.tensor_tensor(c2t, a[:, 1, :], b_[:, 1, :], ALU.mult), reads=ck, writes=["y5"])
                S.op("dve", lambda: vv.tensor_tensor(carry[l][:, 0, :], c1t, c2t, ALU.subtract),
                     reads=["y5"], writes=[f"carry{l}"])
                S.op("dve", lambda: vv.tensor_tensor(c1t, a[:, 0, :], b_[:, 1, :], ALU.mult), reads=ck, writes=["y5"])
                S.op("dve", lambda: vv.tensor_tensor(c2t, a[:, 1, :], b_[:, 0, :], ALU.mult), reads=ck, writes=["y5"])
                S.op("dve", lambda: vv.tensor_tensor(carry[l][:, 1, :], c1t, c2t, ALU.add),
                     reads=["y5"], writes=[f"carry{l}"])
                yield

            gens = [ssd_gen(), s5_gen()]
            while gens:
                for gn in list(gens):
                    try:
                        next(gn)
                    except StopIteration:
                        gens.remove(gn)
            dump("YN", YN[:].rearrange("p a b -> p (a b)"), [f"YN{i}" for i in range(16)])
            dump("gl", gl_[:].rearrange("p a b -> p (a b)"), [f"gl{i}" for i in range(8)])
            S.barrier()
            es_b.close()
            es_a.close()
            ya = sb(es_m, "ya", [128, 8, TT], BF16)
            with ExitStack() as es:
                sga = sb(es, "sga", [128, TT], BF16)
                for half in range(2):
                    wg, wgk = wslab("w_in", l, 0, 8, OFF_GA + 512 * half, 512)
                    wa = [wslab("w_branch_a", l, 0, 16, half * 512 + q * 256, 256) for q in range(2)]
                    for j in range(4):
                        dc = half * 4 + j
                        pk, pt = PS()
                        S.op("pe", lambda: [mm(pt[:], wg[:, k, j * 128:(j + 1) * 128], xn[:, k, :], start=(k == 0),
                                               stop=(k == 7)) for k in range(8)], reads=[wgk] + xk, writes=[pk], c=2.2)
                        S.op("act", lambda: nc.scalar.activation(sga[:], pt[:], AF.Sigmoid), reads=[pk], writes=["sga"])
                        wv, wk = wa[j // 2]
                        pk2, pt2 = PS()
                        S.op("pe", lambda: [mm(pt2[:], wv[:, k, (j % 2) * 128:(j % 2 + 1) * 128], YN[:, k, :],
                                               start=(k == 0), stop=(k == 15)) for k in range(16)],
                             reads=[wk] + [f"YN{i}" for i in range(16)], writes=[pk2], c=4.3)
                        S.op("dve", lambda: nc.vector.tensor_tensor(ya[:, dc, :], pt2[:], sga[:], ALU.mult),
                             reads=[pk2, "sga"], writes=[f"ya{dc}"])
                    wrel(wgk, wa[0][1], wa[1][1])
                V = lambda nm, shp, dt=F32: sb(es, nm, shp, dt)
                glu = V("glu", [128, 8, TT], BF16)
                sgg = V("sgg", [128, TT], BF16)
                glk = [f"gl{i}" for i in range(8)]
                for half in range(2):
                    wv_, wvk = wslab("s5_w_glu", l, 0, 8, 512 * half, 512)
                    wg_, wgk = wslab("s5_w_glu", l, 0, 8, 1024 + 512 * half, 512)
                    for j in range(4):
                        c = half * 4 + j
                        pka, pa = PS()
                        pkb, pb = PS()
                        S.op("pe", lambda: [mm(pa[:], wv_[:, k, j * 128:(j + 1) * 128], gl_[:, k, :], start=(k == 0),
                                               stop=(k == 7)) for k in range(8)], reads=[wvk] + glk, writes=[pka], c=2.2)
                        S.op("pe", lambda: [mm(pb[:], wg_[:, k, j * 128:(j + 1) * 128], gl_[:, k, :], start=(k == 0),
                                               stop=(k == 7)) for k in range(8)], reads=[wgk] + glk, writes=[pkb], c=2.2)
                        S.op("act", lambda: nc.scalar.activation(sgg[:], pb[:], AF.Sigmoid), reads=[pkb], writes=["sgg"])
                        S.op("dve", lambda: vv.tensor_tensor(glu[:, c, :], pa[:], sgg[:], ALU.mult), reads=[pka, "sgg"],
                             writes=[f"glu{c}"])
                    wrel(wvk, wgk)
                sgb = V("sgb", [128, TT], BF16)
                tmpb = V("tmpb", [128, TT])
                gluk = [f"glu{i}" for i in range(8)]
                for half in range(2):
                    wg, wgk = wslab("w_in", l, 0, 8, OFF_GB + 512 * half, 512)
                    wb_, wbk = wslab("w_branch_b", l, 0, 8, 512 * half, 512)
                    for j in range(4):
                        dc = half * 4 + j
                        pk, pt = PS()
                        S.op("pe", lambda: [mm(pt[:], wg[:, k, j * 128:(j + 1) * 128], xn[:, k, :], start=(k == 0),
                                               stop=(k == 7)) for k in range(8)], reads=[wgk] + xk, writes=[pk], c=2.2)
                        S.op("act", lambda: nc.scalar.activation(sgb[:], pt[:], AF.Sigmoid), reads=[pk], writes=["sgb"])
                        pk2, pt2 = PS()
                        S.op("pe", lambda: [mm(pt2[:], wb_[:, k, j * 128:(j + 1) * 128], glu[:, k, :], start=(k == 0),
                                               stop=(k == 7)) for k in range(8)], reads=[wbk] + gluk, writes=[pk2], c=2.2)
                        S.op("dve", lambda: vv.tensor_tensor(tmpb[:], pt2[:], sgb[:], ALU.mult), reads=[pk2, "sgb"],
                             writes=["tmpb"])
                        S.op("dve", lambda: vv.tensor_tensor(ya[:, dc, :], ya[:, dc, :], tmpb[:], ALU.add),
                             reads=["tmpb"], writes=[f"ya{dc}"])
                    wrel(wgk, wbk)
                f = V("f", [128, 8, TT])
                sq2 = V("sq", [128, 8, TT], BF16)
                rs2 = V("rs", [128, TT])
                tmp = V("pntmp", [128, TT])
                proj_out(f, sq2, "w_out", l, ya, [f"ya{c}" for c in range(8)], 8)
                post_norm_residual(f, sq2, rs2, "mix_post_g", l, tmp)
                S.barrier()

    eps_col = pers("eps_col", [128, 1], F32)
    one_col = pers("one_col", [128, 1], F32)
    S.op("dve", lambda: nc.vector.memset(eps_col[:], EPS), writes=["eps_col"])
    S.op("dve", lambda: nc.vector.memset(one_col[:], 1.0), writes=["one_col"])
    setup_consts()
    S.barrier()
    with ExitStack() as es_setup:
        wps = [sb(es_setup, f"wp{l}", [128, 10, 2, 32], F32) for l in range(DEPTH)]
        for l in range(DEPTH):
            setup_s5(l, wps[l])
        setup_tables(wps)
    for tau in range(NTILE):
        load_tile(tau)
        for l in range(DEPTH):
            ffn(l, 1)
            if stop_after == "ffn1":
                break
            mixer(l, tau)
            if stop_after in ("ssd", "s5", "mixer"):
                break
            ffn(l, 2)
            if stop_after == "layer0":
                break
        store_tile(tau)
        if stop_after is not None:
            break
    S.final_wait("sp")
    S.final_wait("act")
    return nc, S


_CONSTS = None


def _consts():
    global _CONSTS
    if _CONSTS is None:
        ident = np.eye(128, dtype=np.float32)
        tri = np.triu(np.ones((128, 128), dtype=np.float32))
        pm = np.zeros((128, 4, 128), dtype=np.float32)
        for pp in range(4):
            for d in range(2):
                gl = 2 * pp + d
                pm[d * 64:(d + 1) * 64, pp, gl * 16:(gl + 1) * 16] = 1.0
        _CONSTS = {"c_ident": ident, "c_tri": tri, "c_pmask": pm.reshape(128, 512)}
    return _CONSTS


def kernel(**inputs):
    nc, _ = build_program()
    x = np.ascontiguousarray(inputs["x"], dtype=np.float32)
    shared = {k: np.ascontiguousarray(inputs[k], dtype=np.float32) for k in PARAM_SHAPES}
    shared.update(_consts())
    in_maps = []
    for b in range(8):
        m = dict(shared)
        m["x"] = x[b]
        in_maps.append(m)
    res = run_bass_kernel_spmd(nc, in_maps, core_ids=list(range(8)))
    return np.stack([np.asarray(r["out"]) for r in res.results], axis=0).astype(np.float32)
```

```python
import math
from contextlib import ExitStack
import numpy as np
import concourse.bass as bass
import concourse.mybir as mybir
from concourse.bass_utils import run_bass_kernel_spmd

F32 = mybir.dt.float32
BF16 = mybir.dt.bfloat16
I32 = mybir.dt.int32
AF = mybir.ActivationFunctionType
ALU = mybir.AluOpType

NTOK = 2048
TT = 512
NTILE = NTOK // TT
D = 1024
FF = 2816
FC = FF // 128
DEPTH = 2
INPROJ = 8224
OFF_Z, OFF_X, OFF_B, OFF_C, OFF_DT, OFF_U5, OFF_GA, OFF_GB = 0, 2048, 4096, 4608, 5120, 5152, 6176, 7200
EPS = 1e-6
TWO_PI = 2.0 * math.pi

PARAM_SHAPES = {
    "ffn1_pre_g": (2, 1024), "ffn1_post_g": (2, 1024), "ffn1_w_gate": (2, 1024, 2816), "ffn1_w_up": (2, 1024, 2816),
    "ffn1_w_down": (2, 2816, 1024), "mix_pre_g": (2, 1024), "mix_post_g": (2, 1024), "w_in": (2, 1024, 8224),
    "ssd_conv_w": (2, 4, 3072), "ssd_conv_b": (2, 3072), "ssd_dt_bias": (2, 32), "ssd_a_log": (2, 32),
    "ssd_d": (2, 32), "ssd_norm_g": (2, 2048), "w_branch_a": (2, 2048, 1024), "s5_lambda_re": (2, 64, 64),
    "s5_lambda_im": (2, 64, 64), "s5_b_re": (2, 64, 64, 16), "s5_b_im": (2, 64, 64, 16), "s5_c_re": (2, 64, 16, 64),
    "s5_c_im": (2, 64, 16, 64), "s5_log_step": (2, 64), "s5_d": (2, 1024), "s5_w_glu": (2, 1024, 2048),
    "w_branch_b": (2, 1024, 1024), "w_out": (2, 1024, 1024), "ffn2_pre_g": (2, 1024), "ffn2_post_g": (2, 1024),
    "ffn2_w_gate": (2, 1024, 2816), "ffn2_w_up": (2, 1024, 2816), "ffn2_w_down": (2, 2816, 1024),
}


class Sched:
    def __init__(self, nc):
        self.nc = nc
        self.e = {"pe": nc.tensor, "act": nc.scalar, "dve": nc.vector, "pool": nc.gpsimd, "sp": nc.sync}
        self.sems = []
        self.semi = {}
        self.cnt = {}
        for k in ("pe", "act", "dve", "pool"):
            self._newsem(k)
        self.waited = {k: {} for k in self.e}
        self.lastw = {}
        self.readers = {}
        self.dmasem = {}
        self.ninstr = {k: 0 for k in self.e}
        self.pending = []
        self.nobarrier = set()
        self.verbose = False
        self.phase_name = ''

    def _alloc(self, name):
        self.sems.append(self.nc.alloc_semaphore(name))
        return len(self.sems) - 1

    def _newsem(self, k):
        self.semi[k] = self._alloc(f"sem_{k}_{len(self.sems)}")
        self.cnt[k] = 0

    def _deps(self, reads, writes):
        toks = []
        for k in reads:
            t = self.lastw.get(k)
            if t is not None:
                toks.append(t)
        for k in writes:
            t = self.lastw.get(k)
            if t is not None:
                toks.append(t)
            r = self.readers.get(k)
            if r:
                toks.extend((si, v, src) for (si, (v, src)) in r.items())
        return toks

    def _waits(self, eng, toks):
        need = {}
        for (si, v, src) in toks:
            if eng == "pe" and src == "pe":
                continue
            if need.get(si, 0) < v:
                need[si] = v
        w = self.waited[eng]
        for si, v in need.items():
            if w.get(si, 0) >= v:
                continue
            self.e[eng].wait_ge(self.sems[si], v)
            w[si] = v

    def _record(self, tok, reads, writes):
        ws = set(writes)
        for k in ws:
            self.lastw[k] = tok
            self.readers[k] = {}
        for k in reads:
            if k in ws:
                continue
            r = self.readers.setdefault(k, {})
            if r.get(tok[0], (0, None))[0] < tok[1]:
                r[tok[0]] = (tok[1], tok[2])

    DEFCOST = {"pe": 1.0, "act": 0.5, "dve": 0.65, "pool": 1.2, "sp": 0.05}

    def op(self, eng, fn, reads=(), writes=(), c=None):
        self.pending.append(("op", eng, _snapshot(fn), tuple(reads), tuple(writes),
                             self.DEFCOST[eng] if c is None else c))

    def dma(self, q, out, in_, reads=(), writes=(), key=None, c=None, persistent=False, **kw):
        key = key or (list(writes)[0] if writes else list(reads)[0])
        if persistent:
            self.nobarrier.add(key)
        self.pending.append(("dma", q, (out, in_, kw, key), tuple(reads), tuple(writes), 3.0 if c is None else c))

    def _emit_op(self, eng, fn, reads, writes):
        self._waits(eng, self._deps(reads, writes))
        ins = fn()
        if isinstance(ins, (list, tuple)):
            self.ninstr[eng] += len(ins)
            ins = ins[-1]
        else:
            self.ninstr[eng] += 1
        self.cnt[eng] += 1
        ins.then_inc(self.sems[self.semi[eng]], 1)
        tok = (self.semi[eng], self.cnt[eng], eng)
        self._record(tok, reads, writes)
        if self.cnt[eng] >= 30000:
            self._newsem(eng)

    def _emit_dma(self, q, out, in_, kw, key, reads, writes):
        self._waits(q, self._deps(reads, writes))
        skey = (key, q == "pool")
        if skey not in self.dmasem:
            self.dmasem[skey] = [self._alloc(f"dsem_{len(self.sems)}"), 0, q]
        ds = self.dmasem[skey]
        if ds[1] >= 30000:
            ds[0] = self._alloc(f"dsem_{len(self.sems)}")
            ds[1] = 0
        ds[1] += 16
        self.e[q].dma_start(out=out, in_=in_, **kw).then_inc(self.sems[ds[0]], 16)
        self.ninstr[q] += 1
        self._record((ds[0], ds[1], "dma"), reads, writes)

    def flush(self):
        import heapq
        ops = self.pending
        self.pending = []
        n = len(ops)
        if n == 0:
            return
        lastw, readers = {}, {}
        preds = [set() for _ in range(n)]
        for i, o in enumerate(ops):
            rd, wr = o[3], o[4]
            for k in rd:
                if k in lastw:
                    preds[i].add(lastw[k])
            for k in wr:
                if k in lastw:
                    preds[i].add(lastw[k])
                for j in readers.get(k, ()):
                    preds[i].add(j)
            ws = set(wr)
            for k in ws:
                lastw[k] = i
                readers[k] = []
            for k in rd:
                if k not in ws:
                    readers.setdefault(k, []).append(i)
            preds[i].discard(i)
        succs = [[] for _ in range(n)]
        npred = [0] * n
        for i in range(n):
            npred[i] = len(preds[i])
            for j in preds[i]:
                succs[j].append(i)
        engs = ("pe", "act", "dve", "pool", "sp")
        tail = [0.0] * n
        for i in range(n - 1, -1, -1):
            t = 0.0
            for j in succs[i]:
                if tail[j] > t:
                    t = tail[j]
            tail[i] = t + ops[i][5] + 0.4
        etime = {e: 0.0 for e in engs}
        fut = {e: [] for e in engs}
        now = {e: [] for e in engs}
        ready_t = [0.0] * n
        finish = [0.0] * n
        for i in range(n):
            if npred[i] == 0:
                heapq.heappush(fut[ops[i][1]], (0.0, i))
        order = []
        LAT = 0.4
        while len(order) < n:
            best = None
            for e in engs:
                f, nw = fut[e], now[e]
                while f and f[0][0] <= etime[e]:
                    k_ = heapq.heappop(f)[1]
                    heapq.heappush(nw, (-tail[k_], k_))
                if nw:
                    cand = (etime[e], nw[0][1], e, True)
                elif f:
                    cand = (f[0][0], f[0][1], e, False)
                else:
                    continue
                if best is None or cand[:2] < best[:2]:
                    best = cand
            st, i, e, isnow = best
            if isnow:
                heapq.heappop(now[e])
            else:
                heapq.heappop(fut[e])
            o = ops[i]
            if o[0] == "dma":
                etime[e] = st + 0.06
                finish[i] = st + o[5]
            else:
                etime[e] = st + o[5]
                finish[i] = etime[e]
            order.append(i)
            for j in succs[i]:
                npred[j] -= 1
                if ready_t[j] < finish[i] + LAT:
                    ready_t[j] = finish[i] + LAT
                if npred[j] == 0:
                    heapq.heappush(fut[ops[j][1]], (ready_t[j], j))
        if self.verbose:
            busy = {e: 0.0 for e in engs}
            for o in ops:
                busy[o[1]] += (0.06 if o[0] == "dma" else o[5])
            print(f"phase {self.phase_name:14s} n={n:5d} makespan {max(finish):8.1f}us  " +
                  " ".join(f"{e}={busy[e]:6.1f}" for e in engs))
        for i in order:
            o = ops[i]
            if o[0] == "op":
                self._emit_op(o[1], o[2], o[3], o[4])
            else:
                out, in_, kw, key = o[2]
                self._emit_dma(o[1], out, in_, kw, key, o[3], o[4])

    def barrier(self, engines=("pe", "act", "dve", "sp"), name=""):
        self.phase_name = name
        self.flush()
        toks = [(self.semi[e], self.cnt[e], e) for e in ("pe", "act", "dve", "pool") if self.cnt[e] > 0]
        toks += [(ds[0], ds[1], "dma") for k, ds in self.dmasem.items() if k[0] not in self.nobarrier and ds[1] > 0]
        for eng in engines:
            self._waits(eng, [t for t in toks if t[2] != eng or eng != "pe"])

    def final_wait(self, eng="sp"):
        self.flush()
        toks = [(self.semi[e], self.cnt[e], e) for e in ("pe", "act", "dve", "pool") if self.cnt[e] > 0]
        toks += [(ds[0], ds[1], "dma") for ds in self.dmasem.values() if ds[1] > 0]
        self._waits(eng, toks)


def _snapshot(fn):
    import types
    if fn.__closure__ is None:
        return fn
    cells = []
    for cl in fn.__closure__:
        try:
            v = cl.cell_contents
        except ValueError:
            cells.append(cl)
            continue
        if isinstance(v, types.FunctionType) and v.__closure__ is not None and v is not fn:
            v = _snapshot(v)
        cells.append(types.CellType(v))
    g = types.FunctionType(fn.__code__, fn.__globals__, fn.__name__, fn.__defaults__, tuple(cells))
    g.__kwdefaults__ = fn.__kwdefaults__
    return g


def build_program(dbg_names=(), stop_after=None):
    nc = bass.Bass("TRN2", target_bir_lowering=False)
    S = Sched(nc)
    T = {}

    def din(name, shape, dt=F32):
        T[name] = nc.dram_tensor(name, list(shape), dt, kind="ExternalInput").ap()
        return T[name]

    x = din("x", [NTOK, D])
    for nm, shp in PARAM_SHAPES.items():
        din(nm, shp)
    c_ident = din("c_ident", [128, 128])
    c_tri = din("c_tri", [128, 128])
    c_pmask = din("c_pmask", [128, 4 * 128])
    out = nc.dram_tensor("out", [NTOK, D], F32, kind="ExternalOutput").ap()
    dbg = {}
    for nm, shp in dbg_names:
        dbg[nm] = nc.dram_tensor("dbg_" + nm, list(shp), F32, kind="ExternalOutput").ap()

    d_tabs = [nc.dram_tensor(f"scr_tabs{l}", [128, 32, 2, 512], BF16, kind="Internal").ap() for l in range(DEPTH)]
    d_bz = [nc.dram_tensor(f"scr_bz{l}", [128, 32, 2, 128], BF16, kind="Internal").ap() for l in range(DEPTH)]
    d_cz = [nc.dram_tensor(f"scr_cz{l}", [128, 32, 3, 128], BF16, kind="Internal").ap() for l in range(DEPTH)]

    uid = [0]

    def sb(es, name, shape, dt):
        uid[0] += 1
        return es.enter_context(nc.sbuf_tensor(f"{name}_{uid[0]}", list(shape), dt))

    def pers(name, shape, dt):
        return nc.alloc_sbuf_tensor(name, list(shape), dt)

    ident_f = pers("ident_f", [128, 128], F32)
    ident_b = pers("ident_b", [128, 128], BF16)
    tri_f = pers("tri_f", [128, 128], F32)
    ones_b = pers("ones_b", [128, 128], BF16)
    ones_f = pers("ones_f", [128, 128], F32)
    pmask = pers("pmask", [128, 4, 128], F32)
    h = pers("h", [128, 8, TT], F32)
    NWB, NWS = 4, 3
    wbuf = {"b": [pers(f"wbufb{i}", [128, 4096], BF16) for i in range(NWB)],
            "s": [pers(f"wbufs{i}", [128, 1024], BF16) for i in range(NWS)]}
    wfree = {"b": list(range(NWB)), "s": list(range(NWS))}
    gcols = {nm: pers("g_" + nm, [128, 2, 8], F32) for nm in
             ("ffn1_pre_g", "ffn1_post_g", "mix_pre_g", "mix_post_g", "ffn2_pre_g", "ffn2_post_g", "s5_d")}
    ng_col = pers("ng_col", [128, 2, 16], F32)
    cw_col = pers("cw_col", [128, 2, 4, 24], F32)
    cb_col = pers("cb_col", [128, 2, 24], F32)
    dsk_col = pers("dsk_col", [128, 2, 16], F32)
    dtb_col = pers("dtb_col", [32, 2], F32)
    aneg_col = pers("aneg_col", [32, 2], F32)
    halo = [pers(f"halo{l}", [128, 24, 3], BF16) for l in range(DEPTH)]
    S32 = [pers(f"S32_{l}", [128, 4, 512], F32) for l in range(DEPTH)]
    Sbf = [pers(f"Sbf_{l}", [128, 4, 512], BF16) for l in range(DEPTH)]
    rho = [pers(f"rho{l}", [128, 32], F32) for l in range(DEPTH)]
    r512 = [pers(f"r512_{l}", [128, 2, 32], F32) for l in range(DEPTH)]
    carry = [pers(f"carry{l}", [128, 2, 32], F32) for l in range(DEPTH)]
    rend = [pers(f"rend{l}", [128, 2, 32], F32) for l in range(DEPTH)]

    ps = [nc.alloc_psum_tensor(f"ps{i}", [128, 512], F32) for i in range(8)]
    psctr = [0]

    reserved = set()

    def PS():
        while True:
            i = psctr[0] % 8
            psctr[0] += 1
            if i not in reserved:
                return f"ps{i}", ps[i]

    def PS_hold():
        while True:
            i = psctr[0] % 8
            psctr[0] += 1
            if i not in reserved:
                reserved.add(i)
                return i, f"ps{i}", ps[i]

    def mm(o, l, r, start=True, stop=True):
        return nc.tensor.matmul(o, l, r, start=start, stop=stop)

    def dump(name, ap, reads):
        if name in dbg:
            S.dma("pool", dbg[name], ap, reads=reads, writes=["dbg_" + name])

    wctr = [0]

    wscr = {}

    def wload(src, kc, ncols, sid):
        pl = "s" if kc * ncols <= 1024 else "b"
        assert wfree[pl], "no free weight slot"
        slot = wfree[pl].pop(0)
        flat = wbuf[pl][slot][:, 0:kc * ncols]
        view = flat.rearrange("p (k f) -> p k f", k=kc)
        key = f"w{pl}{slot}"
        if sid not in wscr:
            wscr[sid] = nc.dram_tensor(f"wscr_{len(wscr)}", [128, kc * ncols], BF16, kind="Internal").ap()
            S.dma("pool", view, src, reads=[], writes=[key], persistent=True)
            S.dma("sp", wscr[sid], flat, reads=[key], writes=["wscr"], key="wscr", persistent=True)
        else:
            S.dma("sp", flat, wscr[sid], reads=["wscr"], writes=[key], key=key, persistent=True)
        return view, key

    def wrel(*keys):
        for key in keys:
            wfree[key[1]].append(int(key[2:]))

    def wslab(name, l, r0, kc, c0, ncols):
        src = T[name][l, r0:r0 + kc * 128, c0:c0 + ncols].rearrange("(k p) f -> p k f", p=128)
        return wload(src, kc, ncols, (name, l, r0, c0, ncols))

    def setup_consts():
        S.dma("sp", ident_f[:], c_ident, writes=["ident_f"])
        S.dma("sp", tri_f[:], c_tri, writes=["tri_f"])
        S.dma("sp", pmask[:], c_pmask.rearrange("p (a b) -> p a b", a=4), writes=["pmask"])
        S.op("dve", lambda: nc.vector.tensor_copy(ident_b[:], ident_f[:]), reads=["ident_f"], writes=["ident_b"])
        S.op("dve", lambda: nc.vector.memset(ones_b[:], 1.0), writes=["ones_b"])
        S.op("dve", lambda: nc.vector.memset(ones_f[:], 1.0), writes=["ones_f"])
        for nm, t in gcols.items():
            S.dma("sp", t[:], T[nm].rearrange("l (c p) -> p l c", p=128), writes=["g_" + nm],
                  allow_slow_non_contiguous=True)
        for nm in ("ffn1_post_g", "ffn2_post_g"):
            S.op("dve", lambda: nc.vector.tensor_scalar(gcols[nm][:], gcols[nm][:], 0.5, None, ALU.mult),
                 reads=[], writes=["g_" + nm])
        S.dma("sp", ng_col[:], T["ssd_norm_g"].rearrange("l (c p) -> p l c", p=128), writes=["ng_col"],
              allow_slow_non_contiguous=True)
        for l in range(DEPTH):
            for k in range(4):
                S.dma("sp", cw_col[:, l, k, :], T["ssd_conv_w"][l, k, :].rearrange("(c p) -> p c", p=128),
                      writes=["cw_col"], allow_slow_non_contiguous=True)
        S.dma("sp", cb_col[:], T["ssd_conv_b"].rearrange("l (c p) -> p l c", p=128), writes=["cb_col"],
              allow_slow_non_contiguous=True)
        dten = T["ssd_d"].tensor
        for hh in range(2):
            src = bass.AP(dten, hh, [[0, 64], [32, 2], [2, 16]])
            S.dma("sp", dsk_col[hh * 64:(hh + 1) * 64, :, :], src, writes=["dsk_col"], allow_slow_non_contiguous=True)
        S.dma("sp", dtb_col[:], T["ssd_dt_bias"].rearrange("l h -> h l"), writes=["dtb_col"],
              allow_slow_non_contiguous=True)
        S.dma("sp", aneg_col[:], T["ssd_a_log"].rearrange("l h -> h l"), writes=["aneg_col"],
              allow_slow_non_contiguous=True)
        S.op("act", lambda: nc.scalar.activation(aneg_col[:], aneg_col[:], AF.Exp), reads=[], writes=["aneg_col"])
        S.op("dve", lambda: nc.vector.tensor_scalar(aneg_col[:], aneg_col[:], -1.0, None, ALU.mult),
             writes=["aneg_col"])
        for l in range(DEPTH):
            S.op("dve", lambda: nc.vector.memset(halo[l][:], 0.0), writes=[f"halo{l}"])
            S.op("dve", lambda: nc.vector.memset(S32[l][:], 0.0), writes=[f"S32_{l}"])
            S.op("dve", lambda: nc.vector.memset(Sbf[l][:], 0.0), writes=[f"Sbf_{l}"])
            S.op("dve", lambda: nc.vector.memset(carry[l][:], 0.0), writes=[f"carry{l}"])

    def setup_s5(l, wp):
        with ExitStack() as es:
            V = lambda nm, shp, dt=F32: sb(es, nm, shp, dt)
            lre, lim, lst = V("lre", [128, 32]), V("lim", [128, 32]), V("lst", [128, 32])
            bre, bim = V("bre", [128, 32, 16]), V("bim", [128, 32, 16])
            cn = [V("cnre", [128, 8, 2, 64]), V("cnim", [128, 8, 2, 64])]
            for d in range(2):
                psl = slice(d * 64, (d + 1) * 64)
                S.dma("sp", lre[psl, :], bass.AP(T["s5_lambda_re"].tensor, l * 4096 + d * 64, [[1, 64], [128, 32]]),
                      writes=["lre"], allow_slow_non_contiguous=True)
                S.dma("sp", lim[psl, :], bass.AP(T["s5_lambda_im"].tensor, l * 4096 + d * 64, [[1, 64], [128, 32]]),
                      writes=["lim"], allow_slow_non_contiguous=True)
                S.dma("sp", lst[psl, :], bass.AP(T["s5_log_step"].tensor, l * 64 + d, [[0, 64], [2, 32]]),
                      writes=["lst"], allow_slow_non_contiguous=True)
                S.dma("sp", bre[psl, :, :], bass.AP(T["s5_b_re"].tensor, l * 65536 + d * 1024,
                                                    [[16, 64], [2048, 32], [1, 16]]), writes=["bre"])
                S.dma("sp", bim[psl, :, :], bass.AP(T["s5_b_im"].tensor, l * 65536 + d * 1024,
                                                    [[16, 64], [2048, 32], [1, 16]]), writes=["bim"])
            for ri, nm in enumerate(("s5_c_re", "s5_c_im")):
                for dup in range(2):
                    S.dma("sp", cn[ri][:, :, dup, :], bass.AP(T[nm].tensor, l * 65536, [[64, 128], [8192, 8], [1, 64]]),
                          writes=[f"cn{ri}"])
            vv = nc.vector
            tmp = [V(f"tmp{i}", [128, 32]) for i in range(8)]
            dl, are, th, cs, sn, br, bi, fre, fim = (V(n, [128, 32]) for n in
                                                     ("dl", "are", "th", "cs", "sn", "br", "bi", "fre", "fim"))
            qi = V("qi", [128, 32], I32)
            K = f"s5set{l}"
            def dve(fn):
                S.op("dve", fn, reads=["lre", "lim", "lst", "bre", "bim"], writes=[K])
            def act(fn):
                S.op("act", fn, reads=[], writes=[K])
            dve(lambda: vv.tensor_scalar(lre[:], lre[:], -1e-4, None, ALU.min))

            def horner(dst, var, coefs):
                dve(lambda: vv.tensor_scalar(dst[:], var[:], coefs[-1], 1.0, ALU.mult, ALU.add))
                for c in reversed(coefs[:-1]):
                    dve(lambda: vv.tensor_tensor(dst[:], dst[:], var[:], ALU.mult))
                    dve(lambda: vv.tensor_scalar(dst[:], dst[:], c, 1.0, ALU.mult, ALU.add))

            dve(lambda: vv.tensor_scalar(tmp[6][:], lst[:], 0.125, None, ALU.mult))
            horner(dl, tmp[6], [1.0 / k for k in range(1, 13)])
            for _ in range(3):
                dve(lambda: vv.tensor_tensor(dl[:], dl[:], dl[:], ALU.mult))
            dve(lambda: vv.tensor_tensor(are[:], lre[:], dl[:], ALU.mult))
            dve(lambda: vv.tensor_tensor(th[:], lim[:], dl[:], ALU.mult))
            em1 = V("em1", [128, 32])
            cm1 = V("cm1", [128, 32])
            horner(tmp[7], are, [1.0 / k for k in range(2, 9)])
            dve(lambda: vv.tensor_tensor(em1[:], are[:], tmp[7][:], ALU.mult))
            dve(lambda: vv.tensor_scalar(rho[l][:], em1[:], 1.0, None, ALU.add))

            def reduce_angle(dst, src, shift):
                dve(lambda: vv.tensor_scalar(tmp[0][:], src[:], shift, 1.0 / TWO_PI, ALU.add, ALU.mult))
                dve(lambda: vv.tensor_copy(qi[:], tmp[0][:]))
                dve(lambda: vv.tensor_copy(tmp[1][:], qi[:]))
                dve(lambda: vv.tensor_scalar(tmp[2][:], src[:], shift, None, ALU.add))
                dve(lambda: vv.scalar_tensor_tensor(dst[:], tmp[1][:], -TWO_PI, tmp[2][:], ALU.mult, ALU.add))
                dve(lambda: vv.tensor_scalar(tmp[3][:], dst[:], math.pi, None, ALU.is_gt))
                dve(lambda: vv.scalar_tensor_tensor(dst[:], tmp[3][:], -TWO_PI, dst[:], ALU.mult, ALU.add))
                dve(lambda: vv.tensor_scalar(tmp[3][:], dst[:], -math.pi, None, ALU.is_lt))
                dve(lambda: vv.scalar_tensor_tensor(dst[:], tmp[3][:], TWO_PI, dst[:], ALU.mult, ALU.add))

            reduce_angle(tmp[4], th, 0.0)
            xq, x2, s1, c1m = tmp[5], tmp[6], tmp[7], tmp[0]
            dve(lambda: vv.tensor_scalar(xq[:], tmp[4][:], 0.25, None, ALU.mult))
            dve(lambda: vv.tensor_tensor(x2[:], xq[:], xq[:], ALU.mult))
            horner(tmp[1], x2, [-1.0 / 6, -1.0 / 20, -1.0 / 42, -1.0 / 72, -1.0 / 110, -1.0 / 156])
            dve(lambda: vv.tensor_tensor(s1[:], xq[:], tmp[1][:], ALU.mult))
            horner(tmp[2], x2, [-1.0 / 12, -1.0 / 30, -1.0 / 56, -1.0 / 90, -1.0 / 132, -1.0 / 182])
            dve(lambda: vv.tensor_tensor(c1m[:], x2[:], tmp[2][:], ALU.mult))
            dve(lambda: vv.tensor_scalar(c1m[:], c1m[:], -0.5, None, ALU.mult))
            for _ in range(2):
                dve(lambda: vv.tensor_scalar(tmp[3][:], c1m[:], 1.0, 2.0, ALU.add, ALU.mult))
                dve(lambda: vv.tensor_tensor(tmp[1][:], s1[:], s1[:], ALU.mult))
                dve(lambda: vv.tensor_tensor(s1[:], s1[:], tmp[3][:], ALU.mult))
                dve(lambda: vv.tensor_scalar(c1m[:], tmp[1][:], -2.0, None, ALU.mult))
            dve(lambda: vv.tensor_copy(sn[:], s1[:]))
            dve(lambda: vv.tensor_copy(cm1[:], c1m[:]))
            dve(lambda: vv.tensor_scalar(cs[:], cm1[:], 1.0, None, ALU.add))
            dve(lambda: vv.tensor_tensor(br[:], em1[:], cs[:], ALU.mult))
            dve(lambda: vv.tensor_tensor(br[:], br[:], cm1[:], ALU.add))
            dve(lambda: vv.tensor_tensor(bi[:], rho[l][:], sn[:], ALU.mult))
            dve(lambda: vv.tensor_tensor(tmp[0][:], lre[:], lre[:], ALU.mult))
            dve(lambda: vv.tensor_tensor(tmp[1][:], lim[:], lim[:], ALU.mult))
            dve(lambda: vv.tensor_tensor(tmp[0][:], tmp[0][:], tmp[1][:], ALU.add))
            dve(lambda: vv.reciprocal(tmp[0][:], tmp[0][:]))
            dve(lambda: vv.tensor_copy(tmp[1][:], br[:]))
            dve(lambda: vv.tensor_tensor(tmp[2][:], tmp[1][:], lre[:], ALU.mult))
            dve(lambda: vv.tensor_tensor(tmp[3][:], bi[:], lim[:], ALU.mult))
            dve(lambda: vv.tensor_tensor(tmp[2][:], tmp[2][:], tmp[3][:], ALU.add))
            dve(lambda: vv.tensor_tensor(fre[:], tmp[2][:], tmp[0][:], ALU.mult))
            dve(lambda: vv.tensor_tensor(tmp[2][:], bi[:], lre[:], ALU.mult))
            dve(lambda: vv.tensor_tensor(tmp[3][:], tmp[1][:], lim[:], ALU.mult))
            dve(lambda: vv.tensor_tensor(tmp[2][:], tmp[2][:], tmp[3][:], ALU.subtract))
            dve(lambda: vv.tensor_tensor(fim[:], tmp[2][:], tmp[0][:], ALU.mult))
            es2 = ExitStack()
            V2 = lambda nm, shp, dt=F32: sb(es2, nm, shp, dt)
            bbr, bbi, t16a, t16b = (V2(n, [128, 32, 16]) for n in ("bbr", "bbi", "t16a", "t16b"))
            fre_b = fre[:].unsqueeze(2).broadcast_to([128, 32, 16])
            fim_b = fim[:].unsqueeze(2).broadcast_to([128, 32, 16])
            dve(lambda: vv.tensor_tensor(t16a[:], bre[:], fre_b, ALU.mult))
            dve(lambda: vv.tensor_tensor(t16b[:], bim[:], fim_b, ALU.mult))
            dve(lambda: vv.tensor_tensor(bbr[:], t16a[:], t16b[:], ALU.subtract))
            dve(lambda: vv.tensor_tensor(t16a[:], bim[:], fre_b, ALU.mult))
            dve(lambda: vv.tensor_tensor(t16b[:], bre[:], fim_b, ALU.mult))
            dve(lambda: vv.tensor_tensor(bbi[:], t16a[:], t16b[:], ALU.add))
            bex = [V2("bexr", [128, 32, 128], BF16), V2("bexi", [128, 32, 128], BF16)]
            bz_sb = V2("bz_sb", [128, 32, 2, 128], BF16)
            for ri, src in enumerate((bbr, bbi)):
                dve(lambda: vv.memset(bex[ri][:], 0.0))
                bex_v = bex[ri][:].rearrange("p (o q) c -> p o q c", q=4)
                src_v = src[:].rearrange("p (o q) c -> p o q c", q=4)
                for pp in range(4):
                    for d in range(2):
                        gl = 2 * pp + d
                        dve(lambda: vv.tensor_copy(bex_v[d * 64:(d + 1) * 64, :, pp, gl * 16:(gl + 1) * 16],
                                                   src_v[d * 64:(d + 1) * 64, :, pp, :]))
            for ri in range(2):
                for pb in range(8):
                    pk, pt = PS()
                    S.op("pe", lambda: [mm(pt[:, j * 128:(j + 1) * 128], bex[ri][:, pb * 4 + j, :], ident_b[:])
                                        for j in range(4)], reads=[K, "ident_b"], writes=[pk])
                    S.op("act", lambda: nc.scalar.copy(bz_sb[:, pb * 4:(pb + 1) * 4, ri, :],
                                                       pt[:].rearrange("p (j m) -> p j m", j=4)),
                         reads=[pk], writes=["bz_sb"])
            S.dma("sp", d_bz[l], bz_sb[:], reads=["bz_sb"], writes=[f"d_bz{l}"])
            cz_sb = V2("cz_sb", [128, 32, 3, 128], BF16)
            for ri in range(2):
                for o in range(8):
                    pk, pt = PS()
                    S.op("pe", lambda: nc.tensor.transpose(pt[:, 0:128],
                                                           cn[ri][:, o, :, :].rearrange("p a n -> p (a n)"),
                                                           ident_f[:]),
                         reads=[f"cn{ri}", "ident_f"], writes=[pk])
                    for var, sgn in (((0, 1.0), (1, -1.0)) if ri == 0 else ((2, -1.0),)):
                        for pp in range(4):
                            S.op("dve", lambda: vv.scalar_tensor_tensor(cz_sb[:, o * 4 + pp, var, :], pt[:, 0:128], sgn,
                                                                        pmask[:, pp, :], ALU.mult, ALU.mult),
                                 reads=[pk, "pmask"], writes=["cz_sb"])
            S.dma("sp", d_cz[l], cz_sb[:], reads=["cz_sb"], writes=[f"d_cz{l}"])
            S.barrier()
            es2.close()
            dve(lambda: vv.tensor_copy(wp[:, 0, 0, :], cs[:]))
            dve(lambda: vv.tensor_copy(wp[:, 0, 1, :], sn[:]))
            for m in range(9):
                dve(lambda: vv.tensor_tensor(tmp[0][:], wp[:, m, 0, :], wp[:, m, 0, :], ALU.mult))
                dve(lambda: vv.tensor_tensor(tmp[1][:], wp[:, m, 1, :], wp[:, m, 1, :], ALU.mult))
                dve(lambda: vv.tensor_tensor(wp[:, m + 1, 0, :], tmp[0][:], tmp[1][:], ALU.subtract))
                dve(lambda: vv.tensor_tensor(tmp[2][:], wp[:, m, 0, :], wp[:, m, 1, :], ALU.mult))
                dve(lambda: vv.tensor_scalar(wp[:, m + 1, 1, :], tmp[2][:], 2.0, None, ALU.mult))
            dve(lambda: vv.tensor_copy(r512[l][:], wp[:, 9, :, :]))
            S.barrier()

    def setup_tables(wps):
        with ExitStack() as es:
            for si, (eng, items) in enumerate((("dve", ((0, 0), (0, 1), (0, 2), (0, 3), (1, 0))),
                                               ("pool", ((1, 1), (1, 2), (1, 3))))):
                ee_ = nc.vector if eng == "dve" else nc.gpsimd
                K = f"tabs{si}"
                tab = sb(es, f"tab{si}", [128, 8, 2, 512], F32)
                ta, tb_ = sb(es, f"ta{si}", [128, 8, 256], F32), sb(es, f"tb{si}", [128, 8, 256], F32)
                for (l, b) in items:
                    wp = wps[l]

                    def do(fn, c=None):
                        S.op(eng, fn, reads=[f"s5set{l}"], writes=[K], c=(c if eng == "dve" else 2 * c))
                    do(lambda: ee_.memset(tab[:, :, 0, 0:1], 1.0), 0.1)
                    do(lambda: ee_.memset(tab[:, :, 1, 0:1], 0.0), 0.1)
                    for m in range(9):
                        n = 1 << m
                        wr = wp[:, m, 0, b * 8:(b + 1) * 8].unsqueeze(2).broadcast_to([128, 8, n])
                        wi = wp[:, m, 1, b * 8:(b + 1) * 8].unsqueeze(2).broadcast_to([128, 8, n])
                        tr0, ti0 = tab[:, :, 0, 0:n], tab[:, :, 1, 0:n]
                        cst = 0.15 + 8 * n * 0.0022
                        do(lambda: ee_.tensor_tensor(ta[:, :, 0:n], tr0, wr, ALU.mult), cst)
                        do(lambda: ee_.tensor_tensor(tb_[:, :, 0:n], ti0, wi, ALU.mult), cst)
                        do(lambda: ee_.tensor_tensor(tab[:, :, 0, n:2 * n], ta[:, :, 0:n], tb_[:, :, 0:n], ALU.subtract), cst)
                        do(lambda: ee_.tensor_tensor(ta[:, :, 0:n], tr0, wi, ALU.mult), cst)
                        do(lambda: ee_.tensor_tensor(tb_[:, :, 0:n], ti0, wr, ALU.mult), cst)
                        do(lambda: ee_.tensor_tensor(tab[:, :, 1, n:2 * n], ta[:, :, 0:n], tb_[:, :, 0:n], ALU.add), cst)
                    S.dma("pool", d_tabs[l][:, b * 8:(b + 1) * 8, :, :], tab[:], reads=[K], writes=[K, f"d_tabs{l}_{b}"],
                          key=f"d_tabs{si}")
            S.barrier()

    def load_tile(tau):
        with ExitStack() as es:
            stage = sb(es, "stage", [128, 4, D], F32)
            for tc in range(4):
                r0 = tau * TT + tc * 128
                S.dma("sp", stage[:, tc, :], x[r0:r0 + 128, :], writes=[f"stage{tc}"])
            for c in range(8):
                pk, pt = PS()
                S.op("pe", lambda: [nc.tensor.transpose(pt[:, tc * 128:(tc + 1) * 128],
                                                        stage[:, tc, c * 128:(c + 1) * 128], ident_f[:])
                                    for tc in range(4)],
                     reads=[f"stage{tc}" for tc in range(4)] + ["ident_f"], writes=[pk])
                S.op("act", lambda: nc.scalar.copy(h[:, c, :], pt[:]), reads=[pk], writes=[f"h{c}"])
            S.barrier()

    def store_tile(tau):
        with ExitStack() as es:
            stage = sb(es, "stage", [128, 4, D], F32)
            for tc in range(4):
                for half in range(2):
                    pk, pt = PS()
                    S.op("pe", lambda: [nc.tensor.transpose(pt[:, j * 128:(j + 1) * 128],
                                                            h[:, half * 4 + j, tc * 128:(tc + 1) * 128], ident_f[:])
                                        for j in range(4)],
                         reads=[f"h{half * 4 + j}" for j in range(4)] + ["ident_f"], writes=[pk])
                    S.op("act", lambda: nc.scalar.copy(stage[:, tc, half * 512:(half + 1) * 512], pt[:]),
                         reads=[pk], writes=[f"stage{tc}"])
                r0 = tau * TT + tc * 128
                S.dma("sp", out[r0:r0 + 128, :], stage[:, tc, :], reads=[f"stage{tc}"], writes=[f"out{tc}"])
            S.barrier()

    def rstd_from_sq(sq, nch, denom, rs, sqkeys):
        pk, pt = PS()
        S.op("pe", lambda: [mm(pt[:], ones_b[:], sq[:, c, :], start=(c == 0), stop=(c == nch - 1))
                            for c in range(nch)], reads=list(sqkeys) + ["ones_b"], writes=[pk])
        S.op("act", lambda: nc.scalar.activation(rs[:], pt[:], AF.Sqrt, bias=eps_col[:], scale=1.0 / denom),
             reads=[pk], writes=["rs"])
        S.op("dve", lambda: nc.vector.reciprocal(rs[:], rs[:]), reads=[], writes=["rs"])

    def norm_pre(xn, sq, rs, gname, l):
        g = gcols[gname]
        for c in range(8):
            S.op("act", lambda: nc.scalar.activation(sq[:, c, :], h[:, c, :], AF.Square),
                 reads=[f"h{c}"], writes=[f"sq{c}"])
        rstd_from_sq(sq, 8, float(D), rs, [f"sq{c}" for c in range(8)])
        for c in range(8):
            S.op("dve", lambda: nc.vector.scalar_tensor_tensor(xn[:, c, :], h[:, c, :], g[:, l, c:c + 1], rs[:],
                                                               ALU.mult, ALU.mult),
                 reads=[f"h{c}", "rs", "g_" + gname], writes=[f"xn{c}"])

    def post_norm_residual(f, sq, rs, gname, l, tmp):
        g = gcols[gname]
        rstd_from_sq(sq, 8, float(D), rs, [f"sq{c}" for c in range(8)])
        for c in range(8):
            S.op("dve", lambda: nc.vector.scalar_tensor_tensor(tmp[:], f[:, c, :], g[:, l, c:c + 1], rs[:],
                                                               ALU.mult, ALU.mult),
                 reads=[f"f{c}", "rs", "g_" + gname], writes=["pn_tmp"])
            S.op("dve", lambda: nc.vector.tensor_tensor(h[:, c, :], h[:, c, :], tmp[:], ALU.add),
                 reads=["pn_tmp"], writes=[f"h{c}"])

    def proj_out(f, sq, wname, l, src, srckeys, kc):
        for ds in range(4):
            wv, wk = wslab(wname, l, 0, kc, ds * 256, 256) if kc <= 16 else (None, None)
            for j in range(2):
                dc = 2 * ds + j
                pk, pt = PS()
                S.op("pe", lambda: [mm(pt[:], wv[:, k, j * 128:(j + 1) * 128], src[:, k, :], start=(k == 0),
                                       stop=(k == kc - 1)) for k in range(kc)],
                     reads=[wk] + list(srckeys), writes=[pk], c=2.2)
                S.op("act", lambda: nc.scalar.copy(f[:, dc, :], pt[:]), reads=[pk], writes=[f"f{dc}"])
                S.op("act", lambda: nc.scalar.activation(sq[:, dc, :], pt[:], AF.Square), reads=[pk],
                     writes=[f"sq{dc}"])
            wrel(wk)

    def ffn(l, which):
        pre = f"ffn{which}"
        with ExitStack() as es:
            xn = sb(es, "xn", [128, 8, TT], BF16)
            sq = sb(es, "sq", [128, 8, TT], BF16)
            rs = sb(es, "rs", [128, TT], F32)
            hb = sb(es, "hb", [128, FC, TT], BF16)
            sg = [sb(es, f"sg{i}", [128, TT], BF16) for i in range(2)]
            f = sb(es, "f", [128, 8, TT], F32)
            tmp = sb(es, "pntmp", [128, TT], F32)
            norm_pre(xn, sq, rs, pre + "_pre_g", l)
            xk = [f"xn{c}" for c in range(8)]
            for fs in range(FC // 2):
                wg, wgk = wslab(pre + "_w_gate", l, 0, 8, fs * 256, 256)
                wu, wuk = wslab(pre + "_w_up", l, 0, 8, fs * 256, 256)
                for j in range(2):
                    fc = 2 * fs + j
                    pka, pa = PS()
                    pkb, pb = PS()
                    S.op("pe", lambda: [mm(pa[:], wg[:, k, j * 128:(j + 1) * 128], xn[:, k, :], start=(k == 0),
                                           stop=(k == 7)) for k in range(8)], reads=[wgk] + xk, writes=[pka], c=2.2)
                    S.op("pe", lambda: [mm(pb[:], wu[:, k, j * 128:(j + 1) * 128], xn[:, k, :], start=(k == 0),
                                           stop=(k == 7)) for k in range(8)], reads=[wuk] + xk, writes=[pkb], c=2.2)
                    sgt = sg[fc % 2]
                    S.op("act", lambda: nc.scalar.activation(sgt[:], pa[:], AF.Silu), reads=[pka],
                         writes=[f"sg{fc % 2}"])
                    S.op("dve", lambda: nc.vector.tensor_tensor(hb[:, fc, :], sgt[:], pb[:], ALU.mult),
                         reads=[f"sg{fc % 2}", pkb], writes=[f"hb{fc}"])
                wrel(wgk, wuk)
            hk = [f"hb{c}" for c in range(FC)]
            for dc in range(8):
                src = T[pre + "_w_down"][l, :, dc * 128:(dc + 1) * 128].rearrange("(k p) f -> p k f", p=128)
                wv, wk = wload(src, FC, 128, (pre + "_w_down", l, dc))
                pk, pt = PS()
                S.op("pe", lambda: [mm(pt[:], wv[:, k, :], hb[:, k, :], start=(k == 0), stop=(k == FC - 1))
                                    for k in range(FC)], reads=[wk] + hk, writes=[pk], c=5.9)
                S.op("act", lambda: nc.scalar.copy(f[:, dc, :], pt[:]), reads=[pk], writes=[f"f{dc}"])
                S.op("act", lambda: nc.scalar.activation(sq[:, dc, :], pt[:], AF.Square), reads=[pk],
                     writes=[f"sq{dc}"])
                wrel(wk)
            post_norm_residual(f, sq, rs, pre + "_post_g", l, tmp)
            S.barrier()

    def mixer(l, tau):
        vv = nc.vector
        with ExitStack() as es_m:
            xn = sb(es_m, "xn", [128, 8, TT], BF16)
            xk = [f"xn{c}" for c in range(8)]
            gl_ = sb(es_m, "gl", [128, 8, TT], BF16)
            YN = sb(es_m, "YN", [128, 16, TT], BF16)
            es_a = ExitStack()
            es_b = ExitStack()
            VA = lambda nm, shp, dt=F32: sb(es_a, nm, shp, dt)
            VB = lambda nm, shp, dt=F32: sb(es_b, nm, shp, dt)
            sq = VA("sq4", [128, 4, TT], BF16)
            rs = VA("rs", [128, TT])
            dtT_full = VA("dtT", [128, TT])
            dtT, adtT = dtT_full[0:32, :], VA("adtT", [32, TT])[:]
            dt_tok, adt_tok, acum_tok = VA("dt_tok", [128, 4, 32]), VA("adt_tok", [128, 4, 32]), VA("acum_tok", [128, 4, 32])
            cd, dstate, ea, dtd = VA("cd", [128, 4, 32]), VA("dstate", [128, 4, 32]), VA("ea", [128, 4, 32]), VA("dtd", [128, 4, 32])
            raw = VA("raw", [128, 6, TT + 3], BF16)
            xc = VA("xc", [128, 6, TT], BF16)
            X_tok, Xd = VA("X_tok", [128, 4, 512], BF16), VA("Xd", [128, 512], BF16)
            B_tok = VA("B_tok", [128, 4, 128], BF16)
            cbm = VA("cbm", [128, 128], BF16)
            dd = VA("dd", [128, 8, 128], BF16)
            ee, mmx = VA("ee", [128, 8, 128], BF16), VA("mmx", [128, 8, 128], BF16)
            yo = VA("yo", [128, 512], BF16)
            yT = VA("yT", [128, 4, TT])
            zs = VA("zs", [128, TT], BF16)
            dg = [VA(f"dg{k}", [128, 128], BF16) for k in range(4)]
            u5 = VB("u5", [128, 8, TT], BF16)
            tb = [VB(f"tbl{i}", [128, 2, 512], BF16) for i in range(2)]
            bzo = [VB(f"bzo{i}", [128, 4, 2, 128], BF16) for i in range(2)]
            czo = [VB(f"czo{i}", [128, 4, 3, 128], BF16) for i in range(2)]
            pA, pB = VB("pA", [128, 2, TT], BF16), VB("pB", [128, 2, TT], BF16)
            QQ = [(VB(f"qA{i}", [128, 2, TT], BF16), VB(f"qB{i}", [128, 2, TT], BF16)) for i in range(2)]
            Wr, Wi = VB("Wr", [128, TT], BF16), VB("Wi", [128, TT], BF16)
            Rr, Ri = VB("Rr", [128, TT], BF16), VB("Ri", [128, TT], BF16)
            y5 = VB("y5", [128, TT])
            g1 = dtT_full

            norm_pre(xn, YN, rs, "mix_pre_g", l)
            wv, wk = wslab("w_in", l, 0, 8, OFF_DT, 32)
            pk, pt = PS()
            S.op("pe", lambda: [mm(pt[0:32, :], wv[:, k, :], xn[:, k, :], start=(k == 0), stop=(k == 7))
                                for k in range(8)], reads=[wk] + xk, writes=[pk], c=2.2)
            wrel(wk)
            S.op("act", lambda: nc.scalar.activation(dtT, pt[0:32, :], AF.Exp, bias=dtb_col[:, l:l + 1]),
                 reads=[pk, "dtb_col"], writes=["dtT"])
            S.op("act", lambda: nc.scalar.activation(dtT, dtT, AF.Ln, bias=one_col[0:32, :]),
                 reads=[], writes=["dtT"])
            S.op("dve", lambda: nc.vector.tensor_scalar(adtT, dtT, aneg_col[:, l:l + 1], None, ALU.mult),
                 reads=["dtT", "aneg_col"], writes=["adtT"])
            pk, pt = PS()
            S.op("pe", lambda: [mm(pt[:, ch * 32:(ch + 1) * 32], dtT_full[0:32, ch * 128:(ch + 1) * 128],
                                   ident_f[0:32, 0:32]) for ch in range(4)] +
                               [mm(pt[:, 128 + ch * 32:128 + (ch + 1) * 32], adtT[:, ch * 128:(ch + 1) * 128],
                                   ident_f[0:32, 0:32]) for ch in range(4)],
                 reads=["dtT", "adtT", "ident_f"], writes=[pk])
            S.op("act", lambda: nc.scalar.copy(dt_tok[:].rearrange("p a b -> p (a b)"), pt[:, 0:128]),
                 reads=[pk], writes=["dt_tok"])
            S.op("act", lambda: nc.scalar.copy(adt_tok[:].rearrange("p a b -> p (a b)"), pt[:, 128:256]),
                 reads=[pk], writes=["adt_tok"])
            pk, pt = PS()
            S.op("pe", lambda: [mm(pt[:, ch * 32:(ch + 1) * 32], tri_f[:], adt_tok[:, ch, :]) for ch in range(4)] +
                               [mm(pt[:, 128 + ch * 32:128 + (ch + 1) * 32], ones_f[:], adt_tok[:, ch, :])
                                for ch in range(4)],
                 reads=["adt_tok", "tri_f", "ones_f"], writes=[pk])
            fl = lambda t: t[:].rearrange("p a b -> p (a b)")
            S.op("act", lambda: nc.scalar.copy(fl(acum_tok), pt[:, 0:128]), reads=[pk], writes=["acum_tok"])
            S.op("act", lambda: nc.scalar.activation(fl(cd), pt[:, 128:256], AF.Exp), reads=[pk], writes=["cd"])
            S.op("dve", lambda: nc.vector.tensor_tensor(fl(dstate), pt[:, 128:256], fl(acum_tok), ALU.subtract),
                 reads=[pk, "acum_tok"], writes=["dstate"])
            S.op("act", lambda: nc.scalar.activation(fl(dstate), fl(dstate), AF.Exp), reads=[], writes=["dstate"])
            S.op("act", lambda: nc.scalar.activation(fl(ea), fl(acum_tok), AF.Exp), reads=["acum_tok"],
                 writes=["ea"])
            S.op("dve", lambda: nc.vector.tensor_tensor(fl(dtd), fl(dt_tok), fl(dstate), ALU.mult),
                 reads=["dt_tok", "dstate"], writes=["dtd"])

            def ssd_gen():
                for g in range(4):
                    chans = [4 * g + i for i in range(4)] + [16 + g, 20 + g]
                    wx, wxk = wslab("w_in", l, 0, 8, OFF_X + 512 * g, 512)
                    wb, wbk = wslab("w_in", l, 0, 8, OFF_B + 128 * g, 128)
                    wc, wck = wslab("w_in", l, 0, 8, OFF_C + 128 * g, 128)
                    for cc in range(6):
                        if cc < 4:
                            wsel, wkk, c0 = wx, wxk, cc * 128
                        elif cc == 4:
                            wsel, wkk, c0 = wb, wbk, 0
                        else:
                            wsel, wkk, c0 = wc, wck, 0
                        ch24 = chans[cc]
                        pk, pt = PS()
                        S.op("pe", lambda: [mm(pt[:], wsel[:, k, c0:c0 + 128], xn[:, k, :], start=(k == 0),
                                               stop=(k == 7)) for k in range(8)], reads=[wkk] + xk, writes=[pk], c=2.2)
                        S.op("act", lambda: nc.scalar.copy(raw[:, cc, 0:3], halo[l][:, ch24, :]),
                             reads=[f"halo{l}_{ch24}"], writes=[f"raw{cc}"], c=0.2)
                        S.op("act", lambda: nc.scalar.copy(raw[:, cc, 3:TT + 3], pt[:]), reads=[pk],
                             writes=[f"raw{cc}"])
                        S.op("act", lambda: nc.scalar.copy(halo[l][:, ch24, :], raw[:, cc, TT:TT + 3]),
                             reads=[f"raw{cc}"], writes=[f"halo{l}_{ch24}"], c=0.2)
                        for k in range(4):
                            S.op("act", lambda: nc.scalar.activation(dg[k][:], ident_b[:], AF.Copy,
                                                                     scale=cw_col[:, l, k, ch24:ch24 + 1]),
                                 reads=["ident_b", "cw_col"], writes=[f"dg{k}"], c=0.25)
                        pk2, pt2 = PS()
                        S.op("pe", lambda: [mm(pt2[:], dg[k][:], raw[:, cc, k:k + TT], start=(k == 0), stop=(k == 3))
                                            for k in range(4)], reads=[f"dg{k}" for k in range(4)] + [f"raw{cc}"],
                             writes=[pk2], c=1.1)
                        S.op("act", lambda: nc.scalar.activation(xc[:, cc, :], pt2[:], AF.Silu,
                                                                 bias=cb_col[:, l, ch24:ch24 + 1]),
                             reads=[pk2, "cb_col"], writes=[f"xc{cc}"])
                        yield
                    wrel(wxk, wbk, wck)
                    for ch in range(4):
                        pk, pt = PS()
                        S.op("pe", lambda: [mm(pt[:, i * 128:(i + 1) * 128], xc[:, i, ch * 128:(ch + 1) * 128],
                                               ident_b[:]) for i in range(4)],
                             reads=[f"xc{i}" for i in range(4)] + ["ident_b"], writes=[pk])
                        ptv = pt[:].rearrange("p (a b) -> p a b", a=8)
                        dtb = dt_tok[:, ch, 8 * g:8 * g + 8].unsqueeze(2).broadcast_to([128, 8, 64])
                        S.op("dve", lambda: nc.vector.tensor_tensor(X_tok[:, ch, :].rearrange("p (a b) -> p a b", a=8),
                                                                    ptv, dtb, ALU.mult),
                             reads=[pk, "dt_tok"], writes=[f"X_tok{ch}"])
                    pk, pt = PS()
                    S.op("pe", lambda: [mm(pt[:, ch * 128:(ch + 1) * 128], xc[:, 4, ch * 128:(ch + 1) * 128],
                                           ident_b[:]) for ch in range(4)], reads=["xc4", "ident_b"], writes=[pk])
                    S.op("act", lambda: nc.scalar.copy(B_tok[:].rearrange("p a b -> p (a b)"), pt[:]), reads=[pk],
                         writes=["B_tok"])
                    yield
                    for ch in range(4):
                        csl = slice(ch * 128, (ch + 1) * 128)
                        pk, pt = PS()
                        S.op("pe", lambda: mm(pt[:, 0:128], xc[:, 4, csl], xc[:, 5, csl]), reads=["xc4", "xc5"],
                             writes=[pk])
                        S.op("dve", lambda: nc.vector.tensor_tensor(cbm[:], pt[:, 0:128], tri_f[:], ALU.mult),
                             reads=[pk, "tri_f"], writes=["cbm"])
                        pka, pa = PS()
                        pkb, pb = PS()
                        S.op("pe", lambda: [mm((pa if hh < 4 else pb)[:, (hh % 4) * 128:(hh % 4 + 1) * 128],
                                               adt_tok[:, ch, 8 * g + hh:8 * g + hh + 1].broadcast_to([128, 128]),
                                               tri_f[:]) for hh in range(8)],
                             reads=["adt_tok", "tri_f"], writes=[pka, pkb])
                        for hh in range(8):
                            src = (pa if hh < 4 else pb)[:, (hh % 4) * 128:(hh % 4 + 1) * 128]
                            S.op("act", lambda: nc.scalar.activation(dd[:, hh, :], src, AF.Relu,
                                                                     bias=acum_tok[:, ch, 8 * g + hh:8 * g + hh + 1],
                                                                     scale=-1.0),
                                 reads=[pka, pkb, "acum_tok"], writes=["dd"])
                        S.op("act", lambda: nc.scalar.activation(ee[:], dd[:], AF.Exp, scale=-1.0), reads=["dd"],
                             writes=["ee"])
                        S.op("pool", lambda: nc.gpsimd.tensor_tensor(mmx[:], ee[:],
                                                                     cbm[:].unsqueeze(1).broadcast_to([128, 8, 128]),
                                                                     ALU.mult), reads=["ee", "cbm"], writes=["mmx"], c=2.4)
                        pk, pt = PS()
                        S.op("pe", lambda: mm(pt[:], xc[:, 5, csl], Sbf[l][:, g, :]), reads=["xc5", f"Sbf{l}_{g}"],
                             writes=[pk])
                        S.op("dve", lambda: nc.vector.tensor_tensor(
                            yo[:].rearrange("p (a b) -> p a b", a=8), pt[:].rearrange("p (a b) -> p a b", a=8),
                            ea[:, ch, 8 * g:8 * g + 8].unsqueeze(2).broadcast_to([128, 8, 64]), ALU.mult),
                             reads=[pk, "ea"], writes=["yo"])
                        pky, py = PS()

                        def ymm():
                            r = []
                            for hp in range(4):
                                cs_ = slice(hp * 128, (hp + 1) * 128)
                                r.append(mm(py[:, cs_], yo[:, cs_], ident_b[:], start=True, stop=False))
                                r.append(mm(py[0:64, cs_], X_tok[:, ch, (2 * hp) * 64:(2 * hp + 1) * 64],
                                            mmx[:, 2 * hp, :], start=False, stop=True))
                                r.append(mm(py[64:128, cs_], X_tok[:, ch, (2 * hp + 1) * 64:(2 * hp + 2) * 64],
                                            mmx[:, 2 * hp + 1, :], start=False, stop=True))
                            return r
                        S.op("pe", ymm, reads=["yo", "ident_b", f"X_tok{ch}", "mmx"], writes=[pky])
                        for hp in range(4):
                            S.op("dve", lambda: nc.vector.scalar_tensor_tensor(
                                yT[:, hp, csl], xc[:, hp, csl], dsk_col[:, l, 4 * g + hp:4 * g + hp + 1],
                                py[:, hp * 128:(hp + 1) * 128], ALU.mult, ALU.add),
                                 reads=[f"xc{hp}", "dsk_col", pky], writes=[f"yT{hp}"])
                        S.op("pool", lambda: nc.gpsimd.tensor_tensor(
                            Xd[:].rearrange("p (a b) -> p a b", a=8), X_tok[:, ch, :].rearrange("p (a b) -> p a b", a=8),
                            dstate[:, ch, 8 * g:8 * g + 8].unsqueeze(2).broadcast_to([128, 8, 64]), ALU.mult),
                             reads=[f"X_tok{ch}", "dstate"], writes=["Xd"], c=1.2)
                        pk, pt = PS()
                        S.op("pe", lambda: mm(pt[:], B_tok[:, ch, :], Xd[:]), reads=["B_tok", "Xd"],
                             writes=[pk])
                        s3 = S32[l][:, g, :].rearrange("p (a b) -> p a b", a=8)
                        S.op("dve", lambda: nc.vector.tensor_tensor(
                            s3, s3, cd[:, ch, 8 * g:8 * g + 8].unsqueeze(2).broadcast_to([128, 8, 64]), ALU.mult),
                             reads=["cd"], writes=[f"S32{l}_{g}"])
                        S.op("dve", lambda: nc.vector.tensor_tensor(S32[l][:, g, :], S32[l][:, g, :], pt[:], ALU.add),
                             reads=[pk], writes=[f"S32{l}_{g}"])
                        S.op("act", lambda: nc.scalar.copy(Sbf[l][:, g, :], S32[l][:, g, :]), reads=[f"S32{l}_{g}"],
                             writes=[f"Sbf{l}_{g}"])
                        yield
                    wz, wzk = wslab("w_in", l, 0, 8, OFF_Z + 512 * g, 512)
                    for zc in range(4):
                        pk, pt = PS()
                        S.op("pe", lambda: [mm(pt[:], wz[:, k, zc * 128:(zc + 1) * 128], xn[:, k, :], start=(k == 0),
                                               stop=(k == 7)) for k in range(8)], reads=[wzk] + xk, writes=[pk], c=2.2)
                        S.op("act", lambda: nc.scalar.activation(zs[:], pt[:], AF.Silu), reads=[pk], writes=["zs"])
                        S.op("pool", lambda: nc.gpsimd.tensor_tensor(yT[:, zc, :], yT[:, zc, :], zs[:], ALU.mult),
                             reads=["zs"], writes=[f"yT{zc}"], c=1.2)
                        S.op("act", lambda: nc.scalar.activation(sq[:, zc, :], yT[:, zc, :], AF.Square),
                             reads=[f"yT{zc}"], writes=[f"sq{zc}"])
                    wrel(wzk)
                    rstd_from_sq(sq, 4, 512.0, rs, [f"sq{c}" for c in range(4)])
                    for zc in range(4):
                        S.op("dve", lambda: nc.vector.scalar_tensor_tensor(
                            YN[:, 4 * g + zc, :], yT[:, zc, :], ng_col[:, l, 4 * g + zc:4 * g + zc + 1], rs[:],
                            ALU.mult, ALU.mult), reads=[f"yT{zc}", "ng_col", "rs"], writes=[f"YN{4 * g + zc}"])
                    yield

            def s5_gen():
                for half in range(2):
                    wv, wk = wslab("w_in", l, 0, 8, OFF_U5 + 512 * half, 512)
                    for j in range(4):
                        c = half * 4 + j
                        pk, pt = PS()
                        S.op("pe", lambda: [mm(pt[:], wv[:, k, j * 128:(j + 1) * 128], xn[:, k, :], start=(k == 0),
                                               stop=(k == 7)) for k in range(8)], reads=[wk] + xk, writes=[pk], c=2.2)
                        S.op("act", lambda: nc.scalar.copy(u5[:, c, :], pt[:]), reads=[pk], writes=[f"u5{c}"])
                    wrel(wk)
                    yield
                for o in range(8):
                    ob = o % 2
                    S.dma("sp", bzo[ob][:], d_bz[l][:, 4 * o:4 * o + 4, :, :], reads=[f"d_bz{l}"], writes=[f"bzo{ob}"])
                    S.dma("sp", czo[ob][:], d_cz[l][:, 4 * o:4 * o + 4, :, :], reads=[f"d_cz{l}"], writes=[f"czo{ob}"])
                    pyi, pky, py = PS_hold()
                    for pp in range(4):
                        P = 4 * o + pp
                        tbi = P % 2
                        tbl = tb[tbi]
                        S.dma("sp", tbl[:], d_tabs[l][:, P, :, :], reads=[f"d_tabs{l}_{P // 8}"], writes=[f"tbl{tbi}"])
                        pkr, pr = PS()
                        pki, pi_ = PS()
                        S.op("pe", lambda: mm(pr[:], bzo[ob][:, pp, 0, :], u5[:, o, :]), reads=[f"bzo{ob}", f"u5{o}"],
                             writes=[pkr])
                        S.op("pe", lambda: mm(pi_[:], bzo[ob][:, pp, 1, :], u5[:, o, :]), reads=[f"bzo{ob}", f"u5{o}"],
                             writes=[pki])
                        tk = f"tbl{tbi}"
                        cT, sT = tbl[:, 0, :], tbl[:, 1, :]
                        bc2 = lambda ap: ap.unsqueeze(1).broadcast_to([128, 2, TT])
                        S.op("dve", lambda: vv.tensor_tensor(pA[:], bc2(pr[:]), tbl[:], ALU.mult), reads=[pkr, tk],
                             writes=["pA"], c=1.15)
                        S.op("dve", lambda: vv.tensor_tensor(pB[:], bc2(pi_[:]), tbl[:], ALU.mult), reads=[pki, tk],
                             writes=["pB"], c=1.15)
                        S.op("pool", lambda: nc.gpsimd.tensor_tensor(Wr[:], pA[:, 0, :], pB[:, 1, :], ALU.add),
                             reads=["pA", "pB"], writes=["Wr"])
                        S.op("pool", lambda: nc.gpsimd.tensor_tensor(Wi[:], pB[:, 0, :], pA[:, 1, :], ALU.subtract),
                             reads=["pA", "pB"], writes=["Wi"])
                        rb = rho[l][:, P:P + 1].broadcast_to([128, TT])
                        S.op("dve", lambda: vv.tensor_tensor_scan(Rr[:], rb, Wr[:], carry[l][:, 0, P:P + 1], ALU.mult,
                                                                  ALU.add), reads=["Wr", f"carry{l}"], writes=["Rr"], c=1.15)
                        S.op("dve", lambda: vv.tensor_tensor_scan(Ri[:], rb, Wi[:], carry[l][:, 1, P:P + 1], ALU.mult,
                                                                  ALU.add), reads=["Wi", f"carry{l}"], writes=["Ri"], c=1.15)
                        S.op("act", lambda: nc.scalar.copy(rend[l][:, 0, P:P + 1], Rr[:, TT - 1:TT]), reads=["Rr"],
                             writes=[f"rend{l}"], c=0.2)
                        S.op("act", lambda: nc.scalar.copy(rend[l][:, 1, P:P + 1], Ri[:, TT - 1:TT]), reads=["Ri"],
                             writes=[f"rend{l}"], c=0.2)
                        qA, qB = QQ[tbi]
                        qk = f"q{tbi}"
                        S.op("dve", lambda: vv.tensor_tensor(qA[:], bc2(Rr[:]), tbl[:], ALU.mult), reads=["Rr", tk],
                             writes=[qk + "A"], c=1.15)
                        S.op("dve", lambda: vv.tensor_tensor(qB[:], bc2(Ri[:]), tbl[:], ALU.mult), reads=["Ri", tk],
                             writes=[qk + "B"], c=1.15)
                        S.op("pe", lambda: [mm(py[:], czo[ob][:, pp, 0, :], qA[:, 0, :], start=(pp == 0), stop=False),
                                            mm(py[:], czo[ob][:, pp, 1, :], qB[:, 1, :], start=False, stop=False),
                                            mm(py[:], czo[ob][:, pp, 2, :], qA[:, 1, :], start=False, stop=False),
                                            mm(py[:], czo[ob][:, pp, 2, :], qB[:, 0, :], start=False, stop=(pp == 3))],
                             reads=[f"czo{ob}", qk + "A", qk + "B"], writes=[pky], c=1.1)
                        yield
                    S.op("dve", lambda: vv.scalar_tensor_tensor(y5[:], u5[:, o, :], gcols["s5_d"][:, l, o:o + 1], py[:],
                                                                ALU.mult, ALU.add), reads=[f"u5{o}", "g_s5_d", pky],
                         writes=["y5"])
                    reserved.discard(pyi)
                    S.op("dve", lambda: vv.tensor_tensor(g1[:], y5[:], y5[:], ALU.mult), reads=["y5"], writes=["dtT"])
                    S.op("dve", lambda: vv.tensor_scalar(g1[:], g1[:], 0.044715, 1.0, ALU.mult, ALU.add), reads=[],
                         writes=["dtT"])
                    S.op("dve", lambda: vv.tensor_tensor(g1[:], g1[:], y5[:], ALU.mult), reads=["y5"], writes=["dtT"])
                    S.op("act", lambda: nc.scalar.activation(g1[:], g1[:], AF.Sigmoid, scale=1.5957691216057308),
                         reads=[], writes=["dtT"])
                    S.op("dve", lambda: vv.tensor_tensor(gl_[:, o, :], y5[:], g1[:], ALU.mult), reads=["y5", "dtT"],
                         writes=[f"gl{o}"])
                    yield
                a, b_ = r512[l], rend[l]
                t1, t2 = p1[:, 0:32].bitcast(BF16) if False else None, None
                ck = [f"rend{l}", f"carry{l}"]
                c1t, c2t = y5[:, 0:32], y5[:, 32:64]
                S.op("dve", lambda: vv.tensor_tensor(c1t, a[:, 0, :], b_[:, 0, :], ALU.mult), reads=ck + ["y5"], writes=["y5"])
                S.op("dve", lambda: vv.tensor_tensor(c2t, a[:, 1, :], b_[:, 1, :], ALU.mult), reads=ck, writes=["y5"])
                S.op("dve", lambda: vv.tensor_tensor(carry[l][:, 0, :], c1t, c2t, ALU.subtract),
                     reads=["y5"], writes=[f"carry{l}"])
                S.op("dve", lambda: vv.tensor_tensor(c1t, a[:, 0, :], b_[:, 1, :], ALU.mult), reads=ck, writes=["y5"])
                S.op("dve", lambda: vv.tensor_tensor(c2t, a[:, 1, :], b_[:, 0, :], ALU.mult), reads=ck, writes=["y5"])
                S.op("dve", lambda: vv.tensor_tensor(carry[l][:, 1, :], c1t, c2t, ALU.add),
                     reads=["y5"], writes=[f"carry{l}"])
                yield

            gens = [ssd_gen(), s5_gen()]
            while gens:
                for gn in list(gens):
                    try:
                        next(gn)
                    except StopIteration:
                        gens.remove(gn)
            dump("YN", YN[:].rearrange("p a b -> p (a b)"), [f"YN{i}" for i in range(16)])
            dump("gl", gl_[:].rearrange("p a b -> p (a b)"), [f"gl{i}" for i in range(8)])
            S.barrier()
            es_b.close()
            es_a.close()
            ya = sb(es_m, "ya", [128, 8, TT], BF16)
            with ExitStack() as es:
                sga = sb(es, "sga", [128, TT], BF16)
                for half in range(2):
                    wg, wgk = wslab("w_in", l, 0, 8, OFF_GA + 512 * half, 512)
                    wa = [wslab("w_branch_a", l, 0, 16, half * 512 + q * 256, 256) for q in range(2)]
                    for j in range(4):
                        dc = half * 4 + j
                        pk, pt = PS()
                        S.op("pe", lambda: [mm(pt[:], wg[:, k, j * 128:(j + 1) * 128], xn[:, k, :], start=(k == 0),
                                               stop=(k == 7)) for k in range(8)], reads=[wgk] + xk, writes=[pk], c=2.2)
                        S.op("act", lambda: nc.scalar.activation(sga[:], pt[:], AF.Sigmoid), reads=[pk], writes=["sga"])
                        wv, wk = wa[j // 2]
                        pk2, pt2 = PS()
                        S.op("pe", lambda: [mm(pt2[:], wv[:, k, (j % 2) * 128:(j % 2 + 1) * 128], YN[:, k, :],
                                               start=(k == 0), stop=(k == 15)) for k in range(16)],
                             reads=[wk] + [f"YN{i}" for i in range(16)], writes=[pk2], c=4.3)
                        S.op("dve", lambda: nc.vector.tensor_tensor(ya[:, dc, :], pt2[:], sga[:], ALU.mult),
                             reads=[pk2, "sga"], writes=[f"ya{dc}"])
                    wrel(wgk, wa[0][1], wa[1][1])
                V = lambda nm, shp, dt=F32: sb(es, nm, shp, dt)
                glu = V("glu", [128, 8, TT], BF16)
                sgg = V("sgg", [128, TT], BF16)
                glk = [f"gl{i}" for i in range(8)]
                for half in range(2):
                    wv_, wvk = wslab("s5_w_glu", l, 0, 8, 512 * half, 512)
                    wg_, wgk = wslab("s5_w_glu", l, 0, 8, 1024 + 512 * half, 512)
                    for j in range(4):
                        c = half * 4 + j
                        pka, pa = PS()
                        pkb, pb = PS()
                        S.op("pe", lambda: [mm(pa[:], wv_[:, k, j * 128:(j + 1) * 128], gl_[:, k, :], start=(k == 0),
                                               stop=(k == 7)) for k in range(8)], reads=[wvk] + glk, writes=[pka], c=2.2)
                        S.op("pe", lambda: [mm(pb[:], wg_[:, k, j * 128:(j + 1) * 128], gl_[:, k, :], start=(k == 0),
                                               stop=(k == 7)) for k in range(8)], reads=[wgk] + glk, writes=[pkb], c=2.2)
                        S.op("act", lambda: nc.scalar.activation(sgg[:], pb[:], AF.Sigmoid), reads=[pkb], writes=["sgg"])
                        S.op("dve", lambda: vv.tensor_tensor(glu[:, c, :], pa[:], sgg[:], ALU.mult), reads=[pka, "sgg"],
                             writes=[f"glu{c}"])
                    wrel(wvk, wgk)
                sgb = V("sgb", [128, TT], BF16)
                tmpb = V("tmpb", [128, TT])
                gluk = [f"glu{i}" for i in range(8)]
                for half in range(2):
                    wg, wgk = wslab("w_in", l, 0, 8, OFF_GB + 512 * half, 512)
                    wb_, wbk = wslab("w_branch_b", l, 0, 8, 512 * half, 512)
                    for j in range(4):
                        dc = half * 4 + j
                        pk, pt = PS()
                        S.op("pe", lambda: [mm(pt[:], wg[:, k, j * 128:(j + 1) * 128], xn[:, k, :], start=(k == 0),
                                               stop=(k == 7)) for k in range(8)], reads=[wgk] + xk, writes=[pk], c=2.2)
                        S.op("act", lambda: nc.scalar.activation(sgb[:], pt[:], AF.Sigmoid), reads=[pk], writes=["sgb"])
                        pk2, pt2 = PS()
                        S.op("pe", lambda: [mm(pt2[:], wb_[:, k, j * 128:(j + 1) * 128], glu[:, k, :], start=(k == 0),
                                               stop=(k == 7)) for k in range(8)], reads=[wbk] + gluk, writes=[pk2], c=2.2)
                        S.op("dve", lambda: vv.tensor_tensor(tmpb[:], pt2[:], sgb[:], ALU.mult), reads=[pk2, "sgb"],
                             writes=["tmpb"])
                        S.op("dve", lambda: vv.tensor_tensor(ya[:, dc, :], ya[:, dc, :], tmpb[:], ALU.add),
                             reads=["tmpb"], writes=[f"ya{dc}"])
                    wrel(wgk, wbk)
                f = V("f", [128, 8, TT])
                sq2 = V("sq", [128, 8, TT], BF16)
                rs2 = V("rs", [128, TT])
                tmp = V("pntmp", [128, TT])
                proj_out(f, sq2, "w_out", l, ya, [f"ya{c}" for c in range(8)], 8)
                post_norm_residual(f, sq2, rs2, "mix_post_g", l, tmp)
                S.barrier()

    eps_col = pers("eps_col", [128, 1], F32)
    one_col = pers("one_col", [128, 1], F32)
    S.op("dve", lambda: nc.vector.memset(eps_col[:], EPS), writes=["eps_col"])
    S.op("dve", lambda: nc.vector.memset(one_col[:], 1.0), writes=["one_col"])
    setup_consts()
    S.barrier()
    with ExitStack() as es_setup:
        wps = [sb(es_setup, f"wp{l}", [128, 10, 2, 32], F32) for l in range(DEPTH)]
        for l in range(DEPTH):
            setup_s5(l, wps[l])
        setup_tables(wps)
    for tau in range(NTILE):
        load_tile(tau)
        for l in range(DEPTH):
            ffn(l, 1)
            if stop_after == "ffn1":
                break
            mixer(l, tau)
            if stop_after in ("ssd", "s5", "mixer"):
                break
            ffn(l, 2)
            if stop_after == "layer0":
                break
        store_tile(tau)
        if stop_after is not None:
            break
    S.final_wait("sp")
    S.final_wait("act")
    return nc, S


_CONSTS = None


def _consts():
    global _CONSTS
    if _CONSTS is None:
        ident = np.eye(128, dtype=np.float32)
        tri = np.triu(np.ones((128, 128), dtype=np.float32))
        pm = np.zeros((128, 4, 128), dtype=np.float32)
        for pp in range(4):
            for d in range(2):
                gl = 2 * pp + d
                pm[d * 64:(d + 1) * 64, pp, gl * 16:(gl + 1) * 16] = 1.0
        _CONSTS = {"c_ident": ident, "c_tri": tri, "c_pmask": pm.reshape(128, 512)}
    return _CONSTS


def kernel(**inputs):
    nc, _ = build_program()
    x = np.ascontiguousarray(inputs["x"], dtype=np.float32)
    shared = {k: np.ascontiguousarray(inputs[k], dtype=np.float32) for k in PARAM_SHAPES}
    shared.update(_consts())
    in_maps = []
    for b in range(8):
        m = dict(shared)
        m["x"] = x[b]
        in_maps.append(m)
    res = run_bass_kernel_spmd(nc, in_maps, core_ids=list(range(8)))
    return np.stack([np.asarray(r["out"]) for r in res.results], axis=0).astype(np.float32)
```
